# Optimizing a Trainium2 kernel written in Bass

```python
import math
import jax, jax.numpy as jnp
from jax import lax
import numpy as np

D_MODEL = 1024
BATCH = 32
SEQ = 2048
DEPTH = 1

N_META = 16
EPS = 1e-6

GLA_HEADS = 4
GLA_DK = 128
GLA_DV = 256
GLA_GATE_RANK = 16
GLA_GATE_NORMALIZER = 16.0
GLA_CHUNK = 64
GLA_KW = GLA_HEADS * GLA_DK
GLA_VW = GLA_HEADS * GLA_DV

MLA_HEADS = 8
MLA_NOPE = 128
MLA_ROPE = 64
MLA_DV = 128
MLA_Q_RANK = 256
MLA_KV_RANK = 128
MLA_QK = MLA_NOPE + MLA_ROPE
MLA_VW = MLA_HEADS * MLA_DV
ROPE_BASE = 10000.0
ATTN_BLOCK = 128

SPLITS = (GLA_KW, GLA_KW, GLA_VW, GLA_GATE_RANK, GLA_VW,
          MLA_Q_RANK, MLA_KV_RANK, MLA_ROPE, MLA_VW,
          D_MODEL, D_MODEL)
IN_WIDTH = (2 * GLA_KW + 2 * GLA_VW + GLA_GATE_RANK + MLA_Q_RANK + MLA_KV_RANK
            + MLA_ROPE + MLA_VW + 2 * D_MODEL)

kernel_name = "hybrid_gla_mla_gated_meta"


def rms_norm(x, g):
    xf = x.astype(jnp.float32)
    y = xf * lax.rsqrt(jnp.mean(xf * xf, axis=-1, keepdims=True) + EPS)
    return (y * g.astype(jnp.float32)).astype(x.dtype)


def rope_tables(n, dim):
    inv = 1.0 / (ROPE_BASE ** (jnp.arange(0, dim, 2, dtype=jnp.float32) / dim))
    ang = jnp.arange(n, dtype=jnp.float32)[:, None] * inv[None, :]
    return jnp.cos(ang), jnp.sin(ang)


def apply_rope(x, cos, sin):
    xf = x.astype(jnp.float32)
    x1, x2 = jnp.split(xf, 2, axis=-1)
    return jnp.concatenate([x1 * cos - x2 * sin, x2 * cos + x1 * sin], axis=-1).astype(x.dtype)


def gla_chunked(q, k, v, g):
    B, L, H, dk = q.shape
    dv = v.shape[-1]
    C = GLA_CHUNK
    front = (-N_META) % C
    back = (-(L - N_META)) % C
    padw = ((0, 0), (front, back), (0, 0), (0, 0))
    q, k, v, g = [jnp.pad(t.astype(jnp.float32), padw) for t in (q, k, v, g)]
    N = q.shape[1] // C

    def chunks(t):
        return t.reshape(B, N, C, H, t.shape[-1]).transpose(1, 0, 3, 2, 4)

    q, k, v, g = chunks(q), chunks(k), chunks(v), chunks(g)
    b = jnp.cumsum(g, axis=3)
    b_last = b[:, :, :, -1:, :]
    qe = q * jnp.exp(b)
    ke = k * jnp.exp(-b)
    kl = k * jnp.exp(b_last - b)
    mask = jnp.tril(jnp.ones((C, C), dtype=bool))
    A = jnp.where(mask, jnp.einsum('nbhid,nbhjd->nbhij', qe, ke), 0.0)
    o_intra = jnp.einsum('nbhij,nbhjv->nbhiv', A, v)
    decay = jnp.exp(b_last[:, :, :, 0, :])

    def step(S, inp):
        qe_n, kl_n, v_n, d_n = inp
        o = jnp.einsum('bhid,bhdv->bhiv', qe_n, S)
        S = S * d_n[..., None] + jnp.einsum('bhjd,bhjv->bhdv', kl_n, v_n)
        return S, o

    S0 = jnp.zeros((B, H, dk, dv), jnp.float32)
    _, o_inter = lax.scan(step, S0, (qe, kl, v, decay))
    o = (o_intra + o_inter).transpose(1, 0, 3, 2, 4).reshape(B, N * C, H, dv)
    return o[:, front:front + L]


def mla_attention(c_q, c_kv, k_rope, q_norm_g, w_uq, kv_norm_g, w_ukv):
    B, L, _ = c_q.shape
    H = MLA_HEADS
    cos, sin = rope_tables(L, MLA_ROPE)
    q = (rms_norm(c_q, q_norm_g) @ w_uq).reshape(B, L, H, MLA_QK)
    q_nope, q_rope = q[..., :MLA_NOPE], q[..., MLA_NOPE:]
    q_rope = apply_rope(q_rope, cos[:, None, :], sin[:, None, :])
    kv = (rms_norm(c_kv, kv_norm_g) @ w_ukv).reshape(B, L, H, MLA_NOPE + MLA_DV)
    k_nope, v = kv[..., :MLA_NOPE], kv[..., MLA_NOPE:]
    k_rope = apply_rope(k_rope, cos, sin)
    k = jnp.concatenate([k_nope, jnp.broadcast_to(k_rope[:, :, None, :], (B, L, H, MLA_ROPE))], axis=-1)
    q = jnp.concatenate([q_nope, q_rope], axis=-1)

    Lp = ((L + ATTN_BLOCK - 1) // ATTN_BLOCK) * ATTN_BLOCK
    nb = Lp // ATTN_BLOCK
    padw = ((0, 0), (0, Lp - L), (0, 0), (0, 0))
    q = jnp.pad(q, padw).transpose(0, 2, 1, 3)
    k = jnp.pad(k, padw).transpose(0, 2, 1, 3)
    v = jnp.pad(v, padw).transpose(0, 2, 1, 3)
    qb = q.reshape(B, H, nb, ATTN_BLOCK, MLA_QK).transpose(2, 0, 1, 3, 4)
    scale = 1.0 / math.sqrt(MLA_QK)
    kpos = jnp.arange(Lp)

    def block(args):
        q_blk, i = args
        s = jnp.einsum('bhqd,bhkd->bhqk', q_blk, k).astype(jnp.float32) * scale
        qpos = i * ATTN_BLOCK + jnp.arange(ATTN_BLOCK)
        s = jnp.where(kpos[None, :] <= qpos[:, None], s, -jnp.inf)
        p = jax.nn.softmax(s, axis=-1)
        return jnp.einsum('bhqk,bhkv->bhqv', p.astype(v.dtype), v)

    o = lax.map(block, (qb, jnp.arange(nb)))
    o = o.transpose(1, 0, 3, 2, 4).reshape(B, Lp, H * MLA_DV)
    return o[:, :L]


def setup_inputs(seed: int = 0) -> dict:
    key = jax.random.key(seed)
    ks = jax.random.split(key, 16)
    f = jnp.float32
    n = lambda k, s, sc: jax.random.normal(k, s, f) * sc
    return {
        "x": n(ks[0], (BATCH, SEQ, D_MODEL), 1.0),
        "meta_tokens": n(ks[1], (N_META, D_MODEL), 1.0),
        "norm_g": 1.0 + n(ks[2], (DEPTH, D_MODEL), 0.02),
        "w_in": n(ks[3], (DEPTH, D_MODEL, IN_WIDTH), D_MODEL ** -0.5),
        "gla_gate_w": n(ks[4], (DEPTH, GLA_GATE_RANK, GLA_KW), GLA_GATE_RANK ** -0.5),
        "gla_gate_b": n(ks[5], (DEPTH, GLA_KW), 0.1),
        "gla_norm_g": 1.0 + n(ks[6], (DEPTH, GLA_DV), 0.02),
        "gla_proj": n(ks[7], (DEPTH, GLA_VW, D_MODEL), GLA_VW ** -0.5),
        "mla_q_norm_g": 1.0 + n(ks[8], (DEPTH, MLA_Q_RANK), 0.02),
        "mla_w_uq": n(ks[9], (DEPTH, MLA_Q_RANK, MLA_HEADS * MLA_QK), MLA_Q_RANK ** -0.5),
        "mla_kv_norm_g": 1.0 + n(ks[10], (DEPTH, MLA_KV_RANK), 0.02),
        "mla_w_ukv": n(ks[11], (DEPTH, MLA_KV_RANK, MLA_HEADS * (MLA_NOPE + MLA_DV)), MLA_KV_RANK ** -0.5),
        "mla_proj": n(ks[12], (DEPTH, MLA_VW, D_MODEL), MLA_VW ** -0.5),
        "w_out": n(ks[13], (DEPTH, D_MODEL, D_MODEL), D_MODEL ** -0.5),
        "final_norm_g": 1.0 + n(ks[14], (D_MODEL,), 0.02),
    }


def reference(x, meta_tokens, norm_g, w_in, gla_gate_w, gla_gate_b, gla_norm_g, gla_proj,
              mla_q_norm_g, mla_w_uq, mla_kv_norm_g, mla_w_ukv, mla_proj, w_out, final_norm_g):
    B = x.shape[0]
    meta = jnp.broadcast_to(meta_tokens[None].astype(x.dtype), (B, N_META, D_MODEL))
    h = jnp.concatenate([meta, x], axis=1)
    L = h.shape[1]
    cuts = [int(c) for c in np.cumsum(SPLITS)[:-1]]
    for l in range(DEPTH):
        u = rms_norm(h, norm_g[l])
        proj = u @ w_in[l]
        (g_q, g_k, g_v, g_lr, g_z, m_cq, m_ckv, m_kr, m_z,
         gate_gla, gate_mla) = jnp.split(proj, cuts, axis=-1)

        q = g_q.reshape(B, L, GLA_HEADS, GLA_DK) * (GLA_DK ** -0.5)
        k = g_k.reshape(B, L, GLA_HEADS, GLA_DK)
        v = g_v.reshape(B, L, GLA_HEADS, GLA_DV)
        gk = jax.nn.log_sigmoid((g_lr @ gla_gate_w[l] + gla_gate_b[l]).astype(jnp.float32)) / GLA_GATE_NORMALIZER
        gk = gk.reshape(B, L, GLA_HEADS, GLA_DK)
        o_a = gla_chunked(q, k, v, gk)
        o_a = rms_norm(o_a, gla_norm_g[l]).reshape(B, L, GLA_VW).astype(h.dtype)
        y_a = (o_a * jax.nn.silu(g_z)) @ gla_proj[l]

        o_b = mla_attention(m_cq, m_ckv, m_kr, mla_q_norm_g[l], mla_w_uq[l],
                            mla_kv_norm_g[l], mla_w_ukv[l])
        y_b = (o_b * jax.nn.silu(m_z)) @ mla_proj[l]

        merged = jax.nn.sigmoid(gate_gla) * y_a + jax.nn.sigmoid(gate_mla) * y_b
        h = h + merged @ w_out[l]
    out = rms_norm(h, final_norm_g)
    return out[:, N_META:]
```

```python
import math
from contextlib import ExitStack

import numpy as np
import concourse.bass as bass
import concourse.mybir as mybir
from concourse.bass_utils import run_bass_kernel_spmd

F32 = mybir.dt.float32
BF16 = mybir.dt.bfloat16
AF = mybir.ActivationFunctionType
ALU = mybir.AluOpType

ENGS = ("pe", "act", "dve", "pool", "sp")


class Buf:
    __slots__ = ("name", "last_w", "readers", "excl")

    def __init__(self, name, excl=False):
        self.name = name
        self.last_w = None
        self.readers = []
        self.excl = excl


class Ins:
    __slots__ = ("eng", "fn", "deps", "needs_inc", "is_dma", "count", "sem_key", "waits", "label")

    def __init__(self, eng, fn, is_dma):
        self.eng = eng
        self.fn = fn
        self.deps = set()
        self.needs_inc = False
        self.is_dma = is_dma
        self.count = None
        self.sem_key = None
        self.waits = None


class Prog:
    def __init__(self):
        self.ins = []
        self.last_dma = {}
        self.label = ""

    def add(self, eng, fn, reads=(), writes=(), dma=None):
        I = Ins(eng, fn, dma is not None)
        I.label = self.label
        if dma is not None:
            I.sem_key = ("dma", dma)
            prev = self.last_dma.get(dma)
            if prev is not None:
                I.deps.add(prev)
            self.last_dma[dma] = I
        ex = [b for b in reads if b.excl]
        if ex:
            reads = [b for b in reads if not b.excl]
            writes = list(writes) + ex
        for b in reads:
            if b.last_w is not None:
                I.deps.add(b.last_w)
        for b in writes:
            if b.last_w is not None:
                I.deps.add(b.last_w)
            for r in b.readers:
                I.deps.add(r)
        for b in reads:
            b.readers.append(I)
        for b in writes:
            b.last_w = I
            b.readers = []
        I.deps.discard(I)
        self.ins.append(I)
        return I

    def finalize(self):
        for I in self.ins:
            if I.eng == "pe" and not I.is_dma:
                I.deps = {d for d in I.deps if not (d.eng == "pe" and not d.is_dma)}
            for d in I.deps:
                d.needs_inc = True
        cnt = {e: 0 for e in ENGS}
        dcnt = {}
        for I in self.ins:
            if I.is_dma:
                dcnt[I.sem_key] = dcnt.get(I.sem_key, 0) + 16
                I.count = dcnt[I.sem_key]
            elif I.needs_inc:
                cnt[I.eng] += 1
                I.sem_key = ("eng", I.eng)
                I.count = cnt[I.eng]
        seen = {e: {} for e in ENGS}
        for I in self.ins:
            w = {}
            for d in I.deps:
                key = d.sem_key
                if w.get(key, 0) < d.count:
                    w[key] = d.count
            s = seen[I.eng]
            out = []
            for key, v in w.items():
                if s.get(key, 0) < v:
                    s[key] = v
                    out.append((key, v))
            I.waits = out
        self.sem_keys = [("eng", e) for e in ENGS] + list(dcnt.keys())
        return cnt, dcnt

    def emit(self, nc, block, sems):
        per = {e: [I for I in self.ins if I.eng == e] for e in ENGS}

        def run(e, eng):
            for I in per[e]:
                for key, v in I.waits:
                    eng.wait_ge(sems[key], v)
                if I.fn is None:
                    continue
                r = I.fn(eng)
                if I.is_dma:
                    r.then_inc(sems[I.sem_key], 16)
                elif I.needs_inc:
                    r.then_inc(sems[I.sem_key], 1)

        @block.tensor
        def _(eng):
            run("pe", eng)

        @block.scalar
        def _(eng):
            run("act", eng)

        @block.vector
        def _(eng):
            run("dve", eng)

        @block.gpsimd
        def _(eng):
            run("pool", eng)

        @block.sync
        def _(eng):
            run("sp", eng)


D = 1024
SEQ = 2048
NMETA = 16
LTOT = SEQ + NMETA
T = 256
NSUB = 2
NCHUNK = 20
WUQ = 19
NRING = 4
EPS = 1e-6
ATT_SCALE = 1.0 / math.sqrt(192.0)
QSCALE = 128.0 ** -0.5

C_ID = 0
C_ONES = 128
C_RESET = 256
C_GMASK = 512
C_TRI = 1024
C_W = 1152


def build(NB, NST, taps=(), stage=9):
    nc = bass.Bass("TRN2", target_bir_lowering=False)
    P = Prog()
    dram_in = lambda n, s: nc.dram_tensor(n, s, F32, kind="ExternalInput").ap()
    x_d = dram_in("x", [NB, SEQ, D])
    meta_d = dram_in("meta", [NMETA, D])
    wst_d = dram_in("wst", [NCHUNK, 128, 4096])
    wsm_d = dram_in("wsm", [128, 1152])
    wuk_d = dram_in("wuk", [128, 2048])
    gwb_d = dram_in("gwb", [17, 512])
    vecs_d = dram_in("vecs", [128, 16])
    fgb_d = dram_in("fgb", [128, D])
    cf_d = dram_in("cf32", [128, C_W])
    rope_d = dram_in("rope", [128, 2, LTOT])
    y_d = nc.dram_tensor("y", [NB, SEQ, D], F32, kind="ExternalOutput").ap()
    wbf_d = nc.dram_tensor("wbf", [NCHUNK, 128, 4096], BF16, kind="Internal").ap()
    sm_d = nc.dram_tensor("smeta", [128, 1024], F32, kind="Internal").ap()
    tap_d = {}
    for name, shape in taps:
        tap_d[name] = nc.dram_tensor("tap_" + name, list(shape), F32, kind="ExternalOutput").ap()

    with ExitStack() as es:
        def sb(name, shape, dt):
            return es.enter_context(nc.sbuf_tensor("sb_" + name, shape, dt))

        def ps(name, shape, dt):
            return es.enter_context(nc.psum_tensor("ps_" + name, shape, dt))

        cf = sb("cf", [128, C_W], F32); Bcf = Buf("cf")
        identb = sb("identb", [128, 128], BF16); Bid = Buf("identb")
        trib = sb("trib", [128, 128], BF16); Btri = Buf("trib")
        vecs = sb("vecs", [128, 16], F32); Bvecs = Buf("vecs")
        fgb = sb("fgb", [128, D], F32); Bfgb = Buf("fgb")
        gwb = sb("gwb", [17, 512], BF16); Bgwb = Buf("gwb")
        wsm = sb("wsm", [128, 8, 144], BF16); Bwsm = Buf("wsm")
        wukT = sb("wukT", [128, 8, 128], BF16); Bwuk = Buf("wukT")
        wuv = sb("wuv", [128, 8, 128], BF16); Bwuv = Buf("wuv")
        ckvnT = sb("ckvnT", [128, LTOT], BF16)
        kropeT = sb("kropeT", [128, LTOT], BF16)
        ckvtok = sb("ckvtok", [128, 17, 130], BF16)
        Bcache = [Buf("cache%d" % j) for j in range(17)]
        S32 = sb("S32", [128, 4, 256], F32)
        S16 = [sb("S16_%d" % i, [128, 4, 256], BF16) for i in range(2)]
        BS32 = [Buf("S32_%d" % h) for h in range(4)]
        BS16 = [[Buf("S16_%d_%d" % (i, h)) for h in range(4)] for i in range(2)]
        BSm = Buf("Sm32")
        ring = [sb("ring%d" % i, [128, 4096], BF16) for i in range(NRING)]
        Bring = [Buf("ring%d" % i) for i in range(NRING)]
        Bscr = [Buf("scr%d" % c) for c in range(NCHUNK)]

        xbuf = [sb("xbuf%d" % i, [128, 2, D], F32) for i in range(2)]
        Bx = [[Buf("x%d_%d" % (i, s)) for s in range(2)] for i in range(2)]
        ropeb = [sb("ropeb%d" % i, [128, 2, T], F32) for i in range(2)]
        Brope = [Buf("rope%d" % i) for i in range(2)]
        junk = sb("junk", [128, D], BF16); Bjunk = Buf("junk")
        reset4 = sb("reset4", [128, 1024], BF16); Breset4 = Buf("reset4")
        stat = sb("stat", [128, 64], F32)
        mhalf = sb("mhalf", [128, 8], F32); Bmh = Buf("mhalf")
        Bst = {}

        def BST(k):
            if k not in Bst:
                Bst[k] = Buf("stat_" + k)
            return Bst[k]
        xn = sb("xn", [128, 2, D], BF16); Bxn = [Buf("xn0"), Buf("xn1")]
        uT = [sb("uT%d" % i, [128, 8, T], BF16) for i in range(2)]
        BuT = [[Buf("uT%d_%d" % (i, s)) for s in range(2)] for i in range(2)]
        uTm = sb("uTm", [128, 8, 16], BF16); BuTm = Buf("uTm")
        qT = sb("qT", [128, 4, T], F32); BqT = [Buf("qT%d" % h) for h in range(4)]
        kT = sb("kT", [128, 4, T], F32); BkT = [Buf("kT%d" % h) for h in range(4)]
        glrT = sb("glrT", [17, T], BF16); Bglr = Buf("glrT")
        vtok = sb("vtok", [128, 2, D], BF16); Bv = [[Buf("v%d_%d" % (s, hf)) for hf in range(2)] for s in range(2)]
        cqT = sb("cqT", [128, 2, T], F32); BcqT = [Buf("cqT0"), Buf("cqT1")]
        cqsq = sb("cqsq", [128, 2, T], F32); Bcqsq = [Buf("cqsq0"), Buf("cqsq1")]
        ckvT = sb("ckvT", [128, T], F32); BckvT = Buf("ckvT")
        ckvsq = sb("ckvsq", [128, T], F32); Bckvsq = Buf("ckvsq")
        rt = [sb("rt%d" % i, [128, T], F32) for i in range(2)]; Brt = [Buf("rt%d" % i) for i in range(2)]
        rt2 = [sb("rt2_%d" % i, [128, 2 * T], F32) for i in range(2)]; Brt2 = [Buf("rt2_%d" % i) for i in range(2)]
        Gb = sb("Gb", [128, 4, T], F32); BGb = [Buf("Gb%d" % h) for h in range(4)]
        BNb = sb("BNb", [128, 4, T], F32); BBN = [Buf("BN%d" % h) for h in range(4)]
        dec = sb("dec", [128, 4, 4], F32); Bdec = [Buf("dec%d" % h) for h in range(4)]
        qeT = sb("qeT", [128, 4, T], BF16); BqeT = [Buf("qeT%d" % h) for h in range(4)]
        keT = sb("keT", [128, 4, T], BF16); BkeT = [Buf("keT%d" % h) for h in range(4)]
        klT = sb("klT", [128, 4, T], BF16); BklT = [Buf("klT%d" % h) for h in range(4)]
        kltok = sb("kltok", [128, 2, 512], BF16); Bkltok = [Buf("kltok0"), Buf("kltok1")]
        ATm = [sb("ATm%d" % i, [128, 512], BF16) for i in range(2)]; BATm = [Buf("ATm0"), Buf("ATm1")]
        siluz = sb("siluz", [128, 2, D], BF16); Bsz = [[Buf("sz%d_%d" % (s, hf)) for hf in range(2)] for s in range(2)]
        Xa = sb("Xa", [128, 2, D], BF16); BXa = [[Buf("Xa%d_%d" % (s, h)) for h in range(4)] for s in range(2)]
        XaT = sb("XaT", [128, 8, T], BF16); BXaT = [Buf("XaT0"), Buf("XaT1")]
        rstdq = sb("rstdq", [128, T], F32); Brq = Buf("rstdq")
        rstdkv = sb("rstdkv", [128, T], F32); Brkv = Buf("rstdkv")
        cqnT = sb("cqnT", [128, 2, T], BF16); Bcqn = [Buf("cqn0"), Buf("cqn1")]
        qnP = [sb("qnP%d" % i, [128, 2 * T], BF16) for i in range(2)]; BqnP = [Buf("qnP0"), Buf("qnP1")]
        qabsT = sb("qabsT", [128, 8, T], BF16); Bqabs = [Buf("qabs%d" % h) for h in range(8)]
        qropeZ = sb("qropeZ", [128, 8, T], BF16); Bqrope = [Buf("qrope%d" % h) for h in range(8)]
        ckvnTm = sb("ckvnTm", [128, 128], BF16)
        kropeTm = sb("kropeTm", [128, 128], BF16)
        pT = [sb("pT%d" % i, [128, 512], BF16) for i in range(4)]; BpT = [Buf("pT%d" % i) for i in range(4)]
        olat = Xa[:, :, :].rearrange("p s (h c) -> p s h c", h=8)
        Bolat = [[BXa[s][h // 2] for h in range(8)] for s in range(2)]
        olatT = sb("olatT", [128, 8, T], BF16); BolatT = [Buf("olatT%d" % h) for h in range(8)]
        dgf = olatT[:, 0:4, :].rearrange("p h t -> p (h t)")
        siluT = [sb("siluT%d" % i, [128, T], F32) for i in range(2)]; BsiluT = [Buf("siluT0"), Buf("siluT1")]
        XbT = sb("XbT", [128, 8, T], BF16); BXbT = [Buf("XbT%d" % h) for h in range(8)]
        sa = sb("sa", [128, 4, T], F32); Bsa = [Buf("sa%d" % c) for c in range(4)]
        sbg = sb("sbg", [128, 4, T], F32); Bsb = [Buf("sb%d" % c) for c in range(4)]
        mT = sb("mT", [128, 8, T], BF16); BmT = [Buf("mT%d" % c) for c in range(8)]
        obuf = [sb("obuf%d" % i, [128, D], F32) for i in range(2)]; Bo = [Buf("o0"), Buf("o1")]

        PB = [ps("PB%d" % i, [128, 512], F32) for i in range(8)]
        BPB = [Buf("PB%d" % i, True) for i in range(8)]
        R = PB[0:4]; BR = BPB[0:4]
        A = PB[4:6]; BA = BPB[4:6]
        TB = [PB[6][:, :].bitcast(BF16), PB[7][:, :].bitcast(BF16)]; BTB = BPB[6:8]
        A4 = [PB[4], PB[5], PB[6], PB[7]]; BA4 = BPB[4:8]
        ctr = {"r": 0, "tb": 0, "a": 0, "a4": 0, "pt": 0, "alt": 0, "o": 0, "u": 0}

        reserved = set()

        def nextR():
            while True:
                i = ctr["r"] % 4
                ctr["r"] += 1
                if i not in reserved:
                    return R[i], BR[i]

        resv8 = set()

        def nextU():
            while True:
                i = ctr["u"] % 8
                ctr["u"] += 1
                if i not in resv8:
                    return PB[i], BPB[i]

        def nextRi():
            while True:
                i = ctr["r"] % 4
                ctr["r"] += 1
                if i not in reserved:
                    return i

        def nextTB():
            i = ctr["tb"] % 2
            ctr["tb"] += 1
            return TB[i], BTB[i]

        def nextA():
            i = ctr["a"] % 2
            ctr["a"] += 1
            return A[i], BA[i]

        def nextA4():
            i = ctr["a4"] % 4
            ctr["a4"] += 1
            return A4[i], BA4[i]

        def nextPT():
            i = ctr["pt"] % 4
            ctr["pt"] += 1
            return pT[i], BpT[i]

        def MM(out, lhsT, rhs, start, stop, reads, writes, sgc=False):
            return P.add("pe", lambda e: e.matmul(out, lhsT=lhsT, rhs=rhs, start=start, stop=stop, skip_group_check=sgc), reads, writes)

        def TR(out, in_, k, reads, writes):
            return P.add("pe", lambda e: e.transpose(out=out, in_=in_, identity=identb[0:k, 0:k]), list(reads) + [Bid], writes)

        def ACT(out, in_, func, reads, writes, bias=None, scale=None, accum=None):
            kw = {}
            if bias is not None:
                kw["bias"] = bias
            if scale is not None:
                kw["scale"] = scale
            if accum is not None:
                kw["accum_out"] = accum
            return P.add("act", lambda e: e.activation(out=out, in_=in_, func=func, **kw), reads, writes)

        def TT(out, in0, in1, op, reads, writes, eng="dve"):
            return P.add(eng, lambda e: e.tensor_tensor(out=out, in0=in0, in1=in1, op=op), reads, writes)

        def TS(out, in0, s1, reads, writes, op0=ALU.mult, s2=None, op1=None, eng="dve"):
            if op1 is None:
                return P.add(eng, lambda e: e.tensor_scalar(out=out, in0=in0, scalar1=s1, scalar2=None, op0=op0), reads, writes)
            return P.add(eng, lambda e: e.tensor_scalar(out=out, in0=in0, scalar1=s1, scalar2=s2, op0=op0, op1=op1), reads, writes)

        def STT(out, in0, scalar, in1, op0, op1, reads, writes):
            return P.add("dve", lambda e: e.scalar_tensor_tensor(out=out, in0=in0, scalar=scalar, in1=in1, op0=op0, op1=op1), reads, writes)

        def CP(eng, out, in_, reads, writes):
            if eng == "act":
                return P.add("act", lambda e: e.activation(out=out, in_=in_, func=AF.Copy), reads, writes)
            return P.add(eng, lambda e: e.tensor_copy(out=out, in_=in_), reads, writes)

        def alt():
            ctr["alt"] += 1
            return "act" if ctr["alt"] % 2 else "dve"

        def DMA(q, out, in_, reads, writes, key):
            return P.add(q, lambda e: e.dma_start(out=out, in_=in_), reads, writes, dma=key)

        def tap(name, src, reads, rows=128):
            if name in tap_d:
                DMA("sp", tap_d[name], src, reads, [], "tap_" + name)

        def rstd_pow(out, in_, k, inv_n, reads, writes):
            rows = out.shape[0]
            TS(out, in_, inv_n, reads, writes, op0=ALU.mult, s2=EPS, op1=ALU.add)
            TT(out, out, mhalf[0:rows, 0:k], ALU.pow, list(writes) + [Bmh], writes, eng="pool")

        def rstd_chain(out, in_, key, inv_n, reads, writes):
            ACT(out, in_, AF.Ln, reads, writes, bias=EPS, scale=inv_n)
            ACT(out, out, AF.Exp, writes, writes, scale=-0.5)

        ring_state = {"pos": 0, "issued": 0, "seq": []}

        def ring_plan(seq):
            ring_state["seq"].extend(seq)

        def ring_issue_upto(p):
            while ring_state["issued"] <= p and ring_state["issued"] < len(ring_state["seq"]):
                i = ring_state["issued"]
                c = ring_state["seq"][i]
                slot = i % NRING
                DMA("sp", ring[slot][:, :], wbf_d[c], [Bscr[c]], [Bring[slot]], "ring%d" % slot)
                ring_state["issued"] += 1

        def ring_acquire(c):
            p = ring_state["pos"]
            while stage < 9 and ring_state["seq"][p] != c:
                ring_state["seq"].pop(p)
            assert ring_state["seq"][p] == c, (p, c, ring_state["seq"][p])
            ring_issue_upto(p + NRING - 1)
            ring_state["pos"] += 1
            slot = p % NRING
            return ring[slot], Bring[slot]

        DMA("sp", cf[:, :], cf_d, [], [Bcf], "cf")
        DMA("sp", vecs[:, :], vecs_d, [], [Bvecs], "vecs")
        DMA("sp", fgb[:, :], fgb_d, [], [Bfgb], "fgb")
        CP("dve", identb[:, :], cf[:, C_ID:C_ID + 128], [Bcf], [Bid])
        TS(trib[:, :], cf[:, C_TRI:C_TRI + 128], -1.0, [Bcf], [Btri], op0=ALU.add, s2=30000.0, op1=ALU.mult)
        P.add("dve", lambda e: e.memset(ckvtok[:, :, 128:130], 1.0), [], Bcache)
        P.add("dve", lambda e: e.memset(ckvtok[:, 0, :], 0.0), [], [Bcache[0]])
        P.add("dve", lambda e: e.memset(ckvtok[0:16, 0, 128:130], 1.0), [], [Bcache[0]])
        P.add("dve", lambda e: e.memset(ckvnTm[:, :], 0.0), [], [Bcache[0]])
        P.add("dve", lambda e: e.memset(kropeTm[:, :], 0.0), [], [Bcache[0]])
        P.add("dve", lambda e: e.memset(qropeZ[:, :, :], 0.0), [], Bqrope)
        P.add("dve", lambda e: e.memset(glrT[:, :], 1.0), [], [Bglr])
        P.add("dve", lambda e: e.memset(mhalf[:, :], -0.5), [], [Bmh])
        P.add("dve", lambda e: e.memset(reset4[:, :], 1.0), [], [Breset4])
        P.add("dve", lambda e: e.memset(reset4[:, :].rearrange("p (c t) -> p c t", t=128)[:, :, 0:1], 0.0), [], [Breset4])
        xflat = [xbuf[i][:, :, :].rearrange("p s d -> p (s d)") for i in range(2)]
        Bxall = [Bx[i] for i in range(2)]
        DMA("sp", xflat[1][:, 0:1152], wsm_d, [], Bxall[1], "x1")
        for kc in range(8):
            TS(wsm[:, kc, :], xflat[1][:, kc * 144:(kc + 1) * 144], vecs[:, kc:kc + 1], Bxall[1] + [Bvecs], [Bwsm])
        DMA("sp", xflat[1][:, 0:2048], wuk_d, [Bwsm], Bxall[1], "x1")
        CP("dve", wukT[:, :, :].rearrange("p h c -> p (h c)"), xflat[1][:, 0:1024], Bxall[1], [Bwuk])
        TS(wuv[:, :, :].rearrange("p h c -> p (h c)"), xflat[1][:, 1024:2048], vecs[:, 12:13], Bxall[1] + [Bvecs], [Bwuv])
        DMA("sp", xflat[1][0:17, 0:512], gwb_d, [Bwuk, Bwuv], Bxall[1], "x1")
        CP("dve", gwb[:, :], xflat[1][0:17, 0:512], Bxall[1], [Bgwb])
        Bpiece = [[Buf("piece%d_%d" % (sl, k)) for k in range(8)] for sl in range(NRING)]
        cast_engs = ["dve", "act", "dve"]
        ce = 0

        def flat4(t):
            return t[:, :, :].rearrange("p a b -> p (a b)")
        stg = [(xflat[0][:, 0:1024], [Bx[0][0]]), (xflat[0][:, 1024:2048], [Bx[0][1]]),
               (xflat[1][:, 0:1024], [Bx[1][0]]), (xflat[1][:, 1024:2048], [Bx[1][1]]),
               (flat4(qT), BqT), (flat4(kT), BkT), (flat4(Gb), BGb), (flat4(BNb), BBN),
               (flat4(sa), Bsa), (flat4(sbg), Bsb), (flat4(S32), BS32)]
        NSTG = len(stg)

        GROUP_A = [1, 2, 3, 4, 0, 5, 6, WUQ, 7, 8]
        GROUP_B = [9, 13, 11, 15, 10, 14, 12, 16, 17, 18]

        def gain_ap(c, q, kc):
            if c == WUQ:
                return vecs[:, 10 + q // 2:11 + q // 2]
            if c <= 12:
                return vecs[:, kc:kc + 1]
            if c <= 14:
                return vecs[:, 8 + (kc % 2):9 + (kc % 2)]
            return None

        def cast_piece(eng, dst, src, sc, rd, wr):
            if sc is None:
                CP(eng, dst, src, rd, wr)
            elif eng == "act":
                ACT(dst, src, AF.Copy, list(rd) + [Bvecs], wr, scale=sc)
            else:
                TS(dst, src, sc, list(rd) + [Bvecs], wr, eng=eng)

        def q_load(i):
            c, q = GROUP_A[i // 4], i % 4
            ap, bufs = stg[i % NSTG]
            DMA("sp", ap, wst_d[c, :, q * 1024:(q + 1) * 1024], [], bufs, "stg%d" % (i % NSTG))

        def q_cast(i):
            nonlocal ce
            ci, q = divmod(i, 4)
            c = GROUP_A[ci]
            slot = ci % NRING
            ap, bufs = stg[i % NSTG]
            for k2 in range(2):
                kc = 2 * q + k2
                eng = cast_engs[ce % 3]
                ce += 1
                cast_piece(eng, ring[slot][:, kc * 512:(kc + 1) * 512], ap[:, k2 * 512:(k2 + 1) * 512], gain_ap(c, q, kc),
                           bufs, [Bpiece[slot][kc]])
            if q == 3:
                DMA("sp", wbf_d[c], ring[slot][:, :], Bpiece[slot] + [Bring[slot]], [Bscr[c]], "ringw%d" % slot)

        NQ = 4 * len(GROUP_A)
        LOOK = NSTG - 1
        for i in range(NQ + LOOK):
            if i < NQ:
                q_load(i)
            if i >= LOOK:
                q_cast(i - LOOK)

        stgB = [(flat4(sa), Bsa), (flat4(sbg), Bsb), (obuf[0][:, :], [Bo[0]]), (obuf[1][:, :], [Bo[1]])]
        bfst = [(mT[:, :, :].rearrange("p a b -> p (a b)"), BmT), (XbT[:, :, :].rearrange("p a b -> p (a b)"), BXbT)]
        NJ = 4 * len(GROUP_B)
        jit_units = []

        def j_load(j):
            c, q = GROUP_B[j // 4], j % 4
            ap, bufs = stgB[j % 4]
            DMA("sp", ap, wst_d[c, :, q * 1024:(q + 1) * 1024], [], bufs, "jl%d" % (j % 4))

        def j_unit(j):
            def f():
                nonlocal ce
                if j + 3 < NJ:
                    j_load(j + 3)
                c, q = GROUP_B[j // 4], j % 4
                ap, bufs = stgB[j % 4]
                hh = j // 2
                bap, bbufs = bfst[hh % 2]
                for k2 in range(2):
                    kc = 2 * q + k2
                    eng = cast_engs[ce % 3]
                    ce += 1
                    o = ((q % 2) * 2 + k2) * 512
                    cast_piece(eng, bap[:, o:o + 512], ap[:, k2 * 512:(k2 + 1) * 512], gain_ap(c, q, kc), bufs, bbufs)
                if q % 2 == 1:
                    DMA("sp", wbf_d[c, :, (q // 2) * 2048:(q // 2 + 1) * 2048], bap, bbufs, [Bscr[c]], "js%d" % (hh % 2))
            return f

        for j in range(3):
            j_load(j)
        for j in range(NJ):
            jit_units.append(j_unit(j))

        def run_jit(k):
            lab = P.label
            P.label = "JIT"
            while k > 0 and jit_units:
                jit_units.pop(0)()
                k -= 1
            P.label = lab

        def phase_norm_T(xi, ui, n_rows_list, meta=False, part=3):
            P.label = "A"
            for s, rows in enumerate(n_rows_list):
                xs = xbuf[xi][0:rows, s, :]
                k = "a%d" % s
                if part & 1:
                    ACT(junk[0:rows, :], xs, AF.Square, [Bx[xi][s]], [Bjunk, BST(k)], accum=stat[0:rows, s:s + 1])
                    rstd_pow(stat[0:rows, 2 + s:3 + s], stat[0:rows, s:s + 1], 1, 1.0 / D, [BST(k)], [BST(k + "r")])
                    TS(xn[0:rows, s, :], xs, stat[0:rows, 2 + s:3 + s], [Bx[xi][s], BST(k + "r")], [Bxn[s]])
                if part & 2:
                    tb, Btb = nextTB()
                    for c in range(8):
                        TR(tb[:, c * rows:(c + 1) * rows], xn[0:rows, s, c * 128:(c + 1) * 128], rows, [Bxn[s]], [Btb])
                    src = tb[:, 0:8 * rows].rearrange("p (c t) -> p c t", c=8)
                    if meta:
                        CP(alt(), uTm[:, :, :], src, [Btb], [BuTm])
                    else:
                        CP(alt(), uT[ui][:, :, s * 128:(s + 1) * 128], src, [Btb], [BuT[ui][s]])

        def proj_fm(wtile, Bw, col0, m, u_ap, Bu, n, ncol=512):
            r, Br = nextR()
            for kc in range(8):
                MM(r[0:m, 0:n], wtile[:, kc * ncol + col0: kc * ncol + col0 + m], u_ap[:, kc, 0:n], kc == 0, kc == 7,
                   [Bw] + Bu, [Br])
            return r, Br

        def gating_all(n, nch, meta):
            cl = n // nch
            if meta:
                for h in range(4):
                    r, Br = nextR()
                    MM(r[:, 0:n], gwb[0:17, h * 128:(h + 1) * 128], glrT[0:17, 0:n], True, True, [Bgwb, Bglr], [Br])
                    ACT(Gb[:, h, 0:n], r[:, 0:n], AF.Exp, [Br], [BGb[h]], scale=-1.0)
                    ACT(Gb[:, h, 0:n], Gb[:, h, 0:n], AF.Ln, [BGb[h]], [BGb[h]], bias=1.0)
                    P.add("dve", lambda e, h=h: e.tensor_tensor_scan(out=BNb[:, h, 0:n], data0=cf[:, C_RESET:C_RESET + n],
                                                                     data1=Gb[:, h, 0:n], initial=0.0, op0=ALU.mult, op1=ALU.add),
                          [Bcf, BGb[h]], [BBN[h]])
                    ACT(Gb[:, h, 0:n], BNb[:, h, 0:n], AF.Exp, [BBN[h]], [BGb[h]], scale=-1.0 / 16)
                    ACT(BNb[:, h, 0:n], BNb[:, h, 0:n], AF.Exp, [BBN[h]], [BBN[h]], scale=1.0 / 16)
                    src = Gb[:, h, 0:n].rearrange("p (c t) -> p c t", t=cl)[:, :, cl - 1:cl]
                    CP("act", dec[:, h, 0:nch].rearrange("p (c o) -> p c o", o=1), src, [BGb[h]], [Bdec[h]])
                return
            Gf = Gb[:, :, :].rearrange("p h t -> p (h t)")
            Bf = BNb[:, :, :].rearrange("p h t -> p (h t)")
            for hp in range(2):
                r, Br = nextR()
                for u in range(2):
                    h = 2 * hp + u
                    MM(r[:, u * T:(u + 1) * T], gwb[0:17, h * 128:(h + 1) * 128], glrT[0:17, 0:T], True, True, [Bgwb, Bglr], [Br])
                ACT(Gf[:, hp * 512:(hp + 1) * 512], r[:, :], AF.Exp, [Br], [BGb[2 * hp], BGb[2 * hp + 1]], scale=-1.0)
            ACT(Gf, Gf, AF.Ln, BGb, BGb, bias=1.0)
            P.add("dve", lambda e: e.tensor_tensor_scan(out=Bf, data0=reset4[:, :], data1=Gf, initial=0.0,
                                                        op0=ALU.mult, op1=ALU.add), [Breset4] + BGb, BBN)
            ACT(Gf, Bf, AF.Exp, BBN, BGb, scale=-1.0 / 16)
            ACT(Bf, Bf, AF.Exp, BBN, BBN, scale=1.0 / 16)
            src = Gf.rearrange("p (h c t) -> p h c t", h=4, c=2)[:, :, :, cl - 1:cl]
            CP("act", dec[:, :, 0:2].rearrange("p h (c o) -> p h c o", o=1), src, BGb, Bdec)

        def gating_part2(n, nch, meta):
            cl = n // nch
            P.label = "B.gate2"
            for h in range(4):
                if not meta:
                    TT(qeT[:, h, 0:n], qT[:, h, 0:n], Gb[:, h, 0:n], ALU.mult, [BqT[h], BGb[h]], [BqeT[h]], eng="pool")
                    TT(keT[:, h, 0:n], kT[:, h, 0:n], BNb[:, h, 0:n], ALU.mult, [BkT[h], BBN[h]], [BkeT[h]], eng="pool")
                for c in range(nch):
                    STT(klT[:, h, c * cl:(c + 1) * cl], kT[:, h, c * cl:(c + 1) * cl], dec[:, h, c:c + 1],
                        BNb[:, h, c * cl:(c + 1) * cl], ALU.mult, ALU.mult, [BkT[h], Bdec[h], BBN[h]], [BklT[h]])

        def phase_B(ui, ri, c0, n, meta=False, hook=None):
            u_ap = uTm if meta else uT[ui]
            Bu = [BuTm] if meta else BuT[ui]
            wsmf = wsm[:, :, :].rearrange("p k c -> p (k c)")
            P.label = "B.small"
            r, Br = proj_fm(wsmf, Bwsm, 128, 16, u_ap, Bu, n, ncol=144)
            CP("act", glrT[0:16, 0:n], r[0:16, 0:n], [Br], [Bglr])
            P.label = "B.q"
            if not meta:
                w, Bw = ring_acquire(0)
                for h in range(4):
                    r, Br = proj_fm(w, Bw, h * 128, 128, u_ap, Bu, n)
                    ACT(qT[:, h, 0:n], r[:, 0:n], AF.Copy, [Br], [BqT[h]], scale=QSCALE)
            if hook: hook()
            P.label = "B.k"
            w, Bw = ring_acquire(1)
            for h in range(4):
                r, Br = proj_fm(w, Bw, h * 128, 128, u_ap, Bu, n)
                CP("dve", kT[:, h, 0:n], r[:, 0:n], [Br], [BkT[h]])
            P.label = "B.small"
            r, Br = proj_fm(wsmf, Bwsm, 0, 128, u_ap, Bu, n, ncol=144)
            TT(rt[1][:, 0:n], r[:, 0:n], ropeb[ri][:, 1, 0:n], ALU.mult, [Br, Brope[ri]], [Brt[1]])
            if hook: hook()
            P.label = "B.c2"
            w, Bw = ring_acquire(2)
            r, Br = proj_fm(w, Bw, 256, 128, u_ap, Bu, n)
            CP("dve", ckvT[:, 0:n], r[:, 0:n], [Br], [BckvT])
            ACT(ckvsq[:, 0:n], r[:, 0:n], AF.Square, [Br], [Bckvsq])
            if not meta:
                for k2 in range(2):
                    r, Br = proj_fm(w, Bw, k2 * 128, 128, u_ap, Bu, n)
                    CP("dve", cqT[:, k2, 0:n], r[:, 0:n], [Br], [BcqT[k2]])
                    ACT(cqsq[:, k2, 0:n], r[:, 0:n], AF.Square, [Br], [Bcqsq[k2]])
            r, Br = proj_fm(w, Bw, 384, 128, u_ap, Bu, n)
            TT(rt[0][:, 0:n], r[:, 0:n], ropeb[ri][:, 0, 0:n], ALU.mult, [Br, Brope[ri]], [Brt[0]])
            if meta:
                TT(kropeTm[:, 0:n], rt[0][:, 0:n], rt[1][:, 0:n], ALU.add, [Brt[0], Brt[1]], [Bcache[0]])
            else:
                kb = [Bcache[1 + (c0 - 16) // 128], Bcache[2 + (c0 - 16) // 128]]
                TT(kropeT[:, c0:c0 + n], rt[0][:, 0:n], rt[1][:, 0:n], ALU.add, [Brt[0], Brt[1]], kb, eng="pool")
            P.label = "B.mla"
            mla_norm_cache(c0, n, meta)
            if not meta:
                r, Br = nextR()
                for k2 in range(2):
                    MM(r[:, 0:n], cf[:, C_ONES:C_ONES + 128], cqsq[:, k2, 0:n], k2 == 0, k2 == 1, [Bcf, Bcqsq[k2]], [Br])
                rstd_chain(rstdq[:, 0:n], r[:, 0:n], "q", 1.0 / 256, [Br], [Brq])
                for k2 in range(2):
                    TT(cqnT[:, k2, :], cqT[:, k2, :], rstdq[:, :], ALU.mult, [BcqT[k2], Brq], [Bcqn[k2]])
            if hook: hook()
            P.label = "B.small"
            gating_all(n, 1 if meta else 2, meta)
            P.label = "B.v"
            for hf in range(2):
                w, Bw = ring_acquire(3 + hf)
                subs = [(0, 16)] if meta else [(0, 128), (1, 128)]
                for (s, rows) in subs:
                    r, Br = nextR()
                    for kc in range(8):
                        MM(r[0:rows, :], u_ap[:, kc, s * 128:s * 128 + rows], w[:, kc * 512:(kc + 1) * 512], kc == 0, kc == 7,
                           [Bw] + Bu, [Br])
                    CP(alt(), vtok[0:rows, s, hf * 512:(hf + 1) * 512], r[0:rows, :], [Br], [Bv[s][hf]])
            if hook: hook()
            P.label = "B.mla"
            mla_cache_tok(c0, n, meta)
            gating_part2(n, 1 if meta else 2, meta)

        def phase_C_gate(n, nch, meta=False):
            P.label = "C.gate"
            subs = [(0, 16)] if meta else [(0, 128), (1, 128)]
            for (s, rows) in subs:
                tb, Btb = nextTB()
                for h in range(4):
                    TR(tb[0:rows, h * 128:(h + 1) * 128], klT[:, h, s * 128:s * 128 + rows], 128, [BklT[h]], [Btb])
                CP(alt(), kltok[0:rows, s, :], tb[0:rows, 0:512], [Btb], [Bkltok[s]])

        def phase_C_meta():
            for h in range(4):
                a, Ba = nextA()
                MM(a[:, 0:256], kltok[0:16, 0, h * 128:(h + 1) * 128], vtok[0:16, 0, h * 256:(h + 1) * 256], True, True,
                   [Bkltok[0], Bv[0][h // 2]], [Ba])
                CP("dve", S32[:, h, :], a[:, 0:256], [Ba], [BS32[h]])

        def phase_C_gz(ui):
            P.label = "C.gz"
            for hf in range(2):
                w, Bw = ring_acquire(5 + hf)
                for s in range(2):
                    r, Br = nextR()
                    for kc in range(8):
                        MM(r[:, :], uT[ui][:, kc, s * 128:(s + 1) * 128], w[:, kc * 512:(kc + 1) * 512], kc == 0, kc == 7,
                           [Bw] + BuT[ui], [Br])
                    ACT(siluz[:, s, hf * 512:(hf + 1) * 512], r[:, :], AF.Silu, [Br], [Bsz[s][hf]])

        def phase_C_core(ui, st, chunk_par):
            for s in range(2):
                P.label = "C.core"
                r, Br = nextR()
                for h in range(4):
                    MM(r[:, h * 128:(h + 1) * 128], keT[:, h, s * 128:(s + 1) * 128], qeT[:, h, s * 128:(s + 1) * 128], True, True,
                       [BkeT[h], BqeT[h]], [Br])
                am, Bam = ATm[s], BATm[s]
                TT(am[:, :], r[:, :], cf[:, C_GMASK:C_GMASK + 512], ALU.mult, [Br, Bcf], [Bam])
                run_units(1)
                pa = (chunk_par + s) % 2
                pb = (pa + 1) % 2
                oi = [nextRi(), nextRi()]
                reserved.update(oi)
                resv8.update(oi)
                obank = [(R[oi[h // 2]][:, (h % 2) * 256:(h % 2) * 256 + 256], BR[oi[h // 2]]) for h in range(4)]
                for h in range(4):
                    r, Br = obank[h]
                    MM(r[:, :], am[:, h * 128:(h + 1) * 128], vtok[:, s, h * 256:(h + 1) * 256], h % 2 == 0, False,
                       [Bam, Bv[s][h // 2]], [Br], sgc=True)
                for h in range(4):
                    r, Br = obank[h]
                    MM(r[:, :], qeT[:, h, s * 128:(s + 1) * 128], S16[pa][:, h, :], False, True,
                       [BqeT[h], BS16[pa][h]], [Br], sgc=True)
                for h in range(4):
                    r, Br = obank[h]
                    ACT(junk[:, 0:256], r[:, :], AF.Square, [Br], [Bjunk, BST("oss%d" % h)], accum=stat[:, 8 + h:9 + h])
                for h in range(4):
                    a, Ba = nextA4()
                    MM(a[:, 0:256], kltok[:, s, h * 128:(h + 1) * 128], vtok[:, s, h * 256:(h + 1) * 256], True, True,
                       [Bkltok[s], Bv[s][h // 2]], [Ba])
                    STT(S32[:, h, :], S32[:, h, :], dec[:, h, s:s + 1], a[:, 0:256], ALU.mult, ALU.add,
                        [BS32[h], Bdec[h], Ba], [BS32[h]])
                    CP("act", S16[pb][:, h, :], S32[:, h, :], [BS32[h]], [BS16[pb][h]])
                for hp in range(2):
                    TT(Xa[:, s, hp * 512:(hp + 1) * 512], R[oi[hp]][:, :], siluz[:, s, hp * 512:(hp + 1) * 512], ALU.mult,
                       [BR[oi[hp]], Bsz[s][hp]], [BXa[s][2 * hp], BXa[s][2 * hp + 1]])
                reserved.clear()
                resv8.clear()
                oss = [BST("oss%d" % h) for h in range(4)]
                rk = "orstd%d" % s
                rstd_pow(stat[:, 40 + 4 * s:44 + 4 * s], stat[:, 8:12], 4, 1.0 / 256, oss, [BST(rk)])
                run_units(1)
                P.label = "C.core"
                for h in range(4):
                    TS(dgf[:, (s * 4 + h) * 128:(s * 4 + h + 1) * 128], identb[:, :], stat[:, 40 + 4 * s + h:41 + 4 * s + h],
                       [Bid, BST(rk)], [BolatT[s * 2 + h // 2]])
                run_units(4)
                run_jit(3)

        def phase_XaT():
            P.label = "C.XaT"
            for s in range(2):
                for c4 in range(2):
                    r, Br = nextR()
                    for cc in range(4):
                        c = c4 * 4 + cc
                        h = c // 2
                        MM(r[:, cc * 128:(cc + 1) * 128], Xa[:, s, c * 128:(c + 1) * 128],
                           dgf[:, (s * 4 + h) * 128:(s * 4 + h + 1) * 128], True, True,
                           [BXa[s][h], BolatT[s * 2 + h // 2]], [Br])
                    CP(alt(), XaT[:, c4 * 4:(c4 + 1) * 4, s * 128:(s + 1) * 128], r[:, :].rearrange("p (c t) -> p c t", c=4),
                       [Br], [BXaT[s]])

        def mla_norm_cache(c0, n, meta=False):
            r, Br = nextR()
            MM(r[:, 0:n], cf[:, C_ONES:C_ONES + 128], ckvsq[:, 0:n], True, True, [Bcf, Bckvsq], [Br])
            rstd_chain(rstdkv[:, 0:n], r[:, 0:n], "kv", 1.0 / 128, [Br], [Brkv])
            if meta:
                kb = [Bcache[0]]
                dstT = ckvnTm[:, 0:n]
            else:
                kb = [Bcache[1 + (c0 - 16) // 128], Bcache[2 + (c0 - 16) // 128]]
                dstT = ckvnT[:, c0:c0 + n]
            TT(dstT, ckvT[:, 0:n], rstdkv[:, 0:n], ALU.mult, [BckvT, Brkv], kb)

        def mla_cache_tok(c0, n, meta=False):
            kb = [Bcache[0]] if meta else [Bcache[1 + (c0 - 16) // 128], Bcache[2 + (c0 - 16) // 128]]
            subs = [(0, 16)] if meta else [(0, 128), (1, 128)]
            for (s, rows) in subs:
                tb, Btb = nextTB()
                src = ckvnTm[:, 0:rows] if meta else ckvnT[:, c0 + s * 128:c0 + s * 128 + rows]
                TR(tb[0:rows, 0:128], src, 128, [kb[s]], [Btb])
                blk = 0 if meta else 1 + (c0 - 16) // 128 + s
                CP(alt(), ckvtok[0:rows, blk, 0:128], tb[0:rows, 0:128], [Btb], [kb[s]])

        units = []

        def run_units(k):
            lab = P.label
            P.label = "D.qproj"
            while k > 0 and units:
                units.pop(0)()
                k -= 1
            P.label = lab

        def make_qproj_units(ri):
            n = T
            wq = {}

            def getw():
                if "w" not in wq:
                    wq["w"] = ring_acquire(WUQ)
                return wq["w"]

            def rope(pr):
                w, Bw = getw()
                a = 1 if pr % 2 == 0 else 3
                r1, Br1 = nextU()
                for k2 in range(2):
                    MM(r1[:, 0:n], w[:, k2 * 2048 + 1024 + pr * 128:k2 * 2048 + 1024 + (pr + 1) * 128], cqnT[:, k2, :],
                       k2 == 0, k2 == 1, [Bw] + Bcqn, [Br1])
                for k2 in range(2):
                    MM(r1[:, n:2 * n], w[:, k2 * 2048 + 1536 + pr * 128:k2 * 2048 + 1536 + (pr + 1) * 128], cqnT[:, k2, :],
                       k2 == 0, k2 == 1, [Bw] + Bcqn, [Br1])
                TT(rt2[a // 2][:, :], r1[:, :], ropeb[ri][:, :, :].rearrange("p c t -> p (c t)"), ALU.mult, [Br1, Brope[ri]], [Brt2[a // 2]])
                TT(qropeZ[0:64, 2 * pr, :], rt2[a // 2][0:64, 0:n], rt2[a // 2][0:64, n:2 * n], ALU.add, [Brt2[a // 2]], [Bqrope[2 * pr]], eng="pool")
                TT(qropeZ[64:128, 2 * pr + 1, :], rt2[a // 2][64:128, 0:n], rt2[a // 2][64:128, n:2 * n], ALU.add, [Brt2[a // 2]],
                   [Bqrope[2 * pr + 1]], eng="pool")

            def q_nope_pair(pr):
                w, Bw = getw()
                r, Br = nextU()
                for u in range(2):
                    h = 2 * pr + u
                    for k2 in range(2):
                        MM(r[:, u * n:(u + 1) * n], w[:, k2 * 2048 + h * 128:k2 * 2048 + (h + 1) * 128], cqnT[:, k2, :], k2 == 0, k2 == 1,
                           [Bw] + Bcqn, [Br])
                qi = pr % 2
                CP("act", qnP[qi][:, :], r[:, :], [Br], [BqnP[qi]])

            def q_absorb_pair(pr):
                qi = pr % 2
                r, Br = nextU()
                for u in range(2):
                    h = 2 * pr + u
                    MM(r[:, u * n:(u + 1) * n], wukT[:, h, :], qnP[qi][:, u * n:(u + 1) * n], True, True, [Bwuk, BqnP[qi]], [Br])
                ACT(qabsT[:, 2 * pr:2 * pr + 2, :].rearrange("p h t -> p (h t)"), r[:, :], AF.Copy, [Br, Bvecs],
                    [Bqabs[2 * pr], Bqabs[2 * pr + 1]], scale=vecs[:, 12:13])

            def mk(f, *a):
                return lambda: f(*a)

            units.append(mk(rope, 0))
            units.append(mk(q_nope_pair, 0))
            units.append(mk(rope, 1))
            units.append(mk(q_nope_pair, 1))
            units.append(mk(q_absorb_pair, 0))
            units.append(mk(rope, 2))
            units.append(mk(q_nope_pair, 2))
            units.append(mk(q_absorb_pair, 1))
            units.append(mk(rope, 3))
            units.append(mk(q_nope_pair, 3))
            units.append(mk(q_absorb_pair, 2))
            units.append(mk(q_absorb_pair, 3))

        def make_gate_units(ui):
            n = T
            wg = {}

            def gate(which, cc):
                def f():
                    lab = P.label
                    P.label = "E.gates"
                    if cc == 0:
                        wg[which] = ring_acquire(9 if which == 0 else 11)
                    w, Bw = wg[which]
                    r, Br = nextU()
                    for kc in range(8):
                        MM(r[:, 0:n], w[:, kc * 512 + cc * 128: kc * 512 + (cc + 1) * 128], uT[ui][:, kc, 0:n], kc == 0, kc == 7,
                           [Bw] + BuT[ui], [Br])
                    if which == 0:
                        ACT(sa[:, cc, :], r[:, 0:n], AF.Sigmoid, [Br], [Bsa[cc]])
                    else:
                        ACT(sbg[:, cc, :], r[:, 0:n], AF.Sigmoid, [Br], [Bsb[cc]])
                    P.label = lab
                return f

            for which in range(2):
                for cc in range(4):
                    units.append(gate(which, cc))

        def phase_D(ui, ri, st):
            c0 = 16 + st * T
            n = T
            flush_sp()
            run_units(100)
            P.label = "D.attn"
            tasks = []
            for h in range(8):
                tasks.append((h, "meta", 0))
                for jp in range(st):
                    tasks.append((h, "pair", jp))
                tasks.append((h, "diag", 0))
            state = {}
            started = {}

            def qk1(r, Br, h, kT_ap, rT_ap, blk, rcol, q0, nq, masked=False):
                MM(r[:, rcol:rcol + nq], kT_ap, qabsT[:, h, q0:q0 + nq], True, False, [Bcache[blk], Bqabs[h]], [Br])
                MM(r[:, rcol:rcol + nq], rT_ap, qropeZ[:, h, q0:q0 + nq], False, not masked, [Bcache[blk], Bqrope[h]], [Br])
                if masked:
                    MM(r[:, rcol:rcol + 128], identb[:, :], trib[:, :], False, True, [Bid, Btri], [Br])

            def emit_qk(t):
                h, kind, jp = t
                r, Br = nextR()
                p, Bp = nextPT()
                if kind == "meta":
                    qk1(r, Br, h, ckvnTm[:, :], kropeTm[:, :], 0, 0, 0, T)
                    ACT(p[:, 0:T], r[:, 0:T], AF.Exp, [Br], [Bp], scale=ATT_SCALE)
                elif kind == "pair":
                    for u in range(2):
                        j = 2 * jp + u
                        kc0 = 16 + 128 * j
                        qk1(r, Br, h, ckvnT[:, kc0:kc0 + 128], kropeT[:, kc0:kc0 + 128], 1 + j, u * 256, 0, T)
                    ACT(p[:, :], r[:, :], AF.Exp, [Br], [Bp], scale=ATT_SCALE)
                else:
                    j0 = 2 * st
                    kc0 = 16 + 128 * j0
                    qk1(r, Br, h, ckvnT[:, kc0:kc0 + 128], kropeT[:, kc0:kc0 + 128], 1 + j0, 0, 0, T, masked=True)
                    qk1(r, Br, h, ckvnT[:, kc0 + 128:kc0 + 256], kropeT[:, kc0 + 128:kc0 + 256], 2 + j0, 384, 128, 128, masked=True)
                    ACT(p[:, 0:256], r[:, 0:256], AF.Exp, [Br], [Bp], scale=ATT_SCALE)
                    ACT(p[:, 384:512], r[:, 384:512], AF.Exp, [Br], [Bp], scale=ATT_SCALE)
                state[t] = (p, Bp)

            def pv(h, p, Bp, pcol, sq, blk, last):
                ai = 2 * (h % 2) + sq
                MM(A4[ai][:, 0:129], p[:, pcol:pcol + 128], ckvtok[:, blk, 0:129], not started.get((h, sq), False), last,
                   [Bp, Bcache[blk]], [BA4[ai]])
                started[(h, sq)] = True

            def emit_pv(t):
                h, kind, jp = t
                p, Bp = state.pop(t)
                if kind == "meta":
                    for sq in range(2):
                        pv(h, p, Bp, sq * 128, sq, 0, False)
                elif kind == "pair":
                    for u in range(2):
                        j = 2 * jp + u
                        for sq in range(2):
                            pv(h, p, Bp, u * 256 + sq * 128, sq, 1 + j, False)
                else:
                    j0 = 2 * st
                    pv(h, p, Bp, 0, 0, 1 + j0, True)
                    pv(h, p, Bp, 128, 1, 1 + j0, False)
                    pv(h, p, Bp, 384, 1, 2 + j0, True)
                    for sq in range(2):
                        ai = 2 * (h % 2) + sq
                        k = "rec%d" % ai
                        col = 16 + sq * 8 + h
                        P.add("dve", lambda e, ai=ai, col=col: e.reciprocal(out=stat[:, col:col + 1], in_=A4[ai][:, 128:129]),
                              [BA4[ai]], [BST(k)])
                        TS(olat[:, sq, h, :], A4[ai][:, 0:128], stat[:, col:col + 1], [BA4[ai], BST(k)], [Bolat[sq][h]])

            DEPTH = 3
            for i, t in enumerate(tasks):
                if i == 2:
                    phase_XaT()
                    P.label = "D.attn"
                run_jit(1)
                emit_qk(t)
                if i >= DEPTH:
                    emit_pv(tasks[i - DEPTH])
            for t in tasks[-DEPTH:]:
                emit_pv(t)
            run_jit(1000)
            P.label = "D.up"
            wz = {}

            def up_a(h):
                if h % 4 == 0:
                    wz["w"] = ring_acquire(7 + h // 4)
                w, Bw = wz["w"]
                tb, Btb = nextTB()
                for sq in range(2):
                    TR(tb[:, sq * 128:(sq + 1) * 128], olat[:, sq, h, :], 128, [Bolat[sq][h]], [Btb])
                CP("act", olatT[:, h, :], tb[:, 0:T], [Btb], [BolatT[h]])
                r, Br = proj_fm(w, Bw, (h % 4) * 128, 128, uT[ui], BuT[ui], n)
                si = h % 2
                ACT(siluT[si][:, :], r[:, 0:n], AF.Silu, [Br], [BsiluT[si]])

            def up_b(h):
                si = h % 2
                r2, Br2 = nextR()
                MM(r2[:, 0:n], wuv[:, h, :], olatT[:, h, :], True, True, [Bwuv, BolatT[h]], [Br2])
                TT(XbT[:, h, :], r2[:, 0:n], siluT[si][:, :], ALU.mult, [Br2, BsiluT[si]], [BXbT[h]])

            for h in range(8):
                up_a(h)
                if h >= 1:
                    up_b(h - 1)
            up_b(7)

        def phase_E(xi, ui, b, st, pre_gates=False):
            n = T
            P.label = "E.gates"
            for hf in range(2):
                if not (pre_gates and hf == 0):
                    w, Bw = ring_acquire(9 + hf)
                    for cc in range(4):
                        r, Br = proj_fm(w, Bw, cc * 128, 128, uT[ui], BuT[ui], n)
                        ACT(sa[:, cc, :], r[:, 0:n], AF.Sigmoid, [Br], [Bsa[cc]])
                w, Bw = ring_acquire(13 + hf)
                for cc in range(4):
                    r, Br = proj_fm(w, Bw, cc * 128, 128, XaT, BXaT, n)
                    TT(sa[:, cc, :], r[:, 0:n], sa[:, cc, :], ALU.mult, [Br, Bsa[cc]], [Bsa[cc]])
                if not (pre_gates and hf == 0):
                    w, Bw = ring_acquire(11 + hf)
                    for cc in range(4):
                        r, Br = proj_fm(w, Bw, cc * 128, 128, uT[ui], BuT[ui], n)
                        ACT(sbg[:, cc, :], r[:, 0:n], AF.Sigmoid, [Br], [Bsb[cc]])
                w, Bw = ring_acquire(15 + hf)
                for cc in range(4):
                    r, Br = proj_fm(w, Bw, cc * 128, 128, XbT, BXbT, n)
                    TT(sbg[:, cc, :], r[:, 0:n], sbg[:, cc, :], ALU.mult, [Br, Bsb[cc]], [Bsb[cc]])
                    TT(mT[:, hf * 4 + cc, :], sa[:, cc, :], sbg[:, cc, :], ALU.add, [Bsa[cc], Bsb[cc]], [BmT[hf * 4 + cc]])
            P.label = "E.wout"
            for hf in range(2):
                w, Bw = ring_acquire(17 + hf)
                for s in range(2):
                    r, Br = nextR()
                    for kc in range(8):
                        MM(r[:, :], mT[:, kc, s * 128:(s + 1) * 128], w[:, kc * 512:(kc + 1) * 512], kc == 0, kc == 7,
                           [Bw] + BmT, [Br])
                    xs = xbuf[xi][:, s, hf * 512:(hf + 1) * 512]
                    TT(xs, xs, r[:, :], ALU.add, [Bx[xi][s], Br], [Bx[xi][s]])
            P.label = "E.final"
            for s in range(2):
                k = "f%d" % s
                ACT(junk[:, :], xbuf[xi][:, s, :], AF.Square, [Bx[xi][s]], [Bjunk, BST(k)], accum=stat[:, 32 + s:33 + s])
                rstd_pow(stat[:, 34 + s:35 + s], stat[:, 32 + s:33 + s], 1, 1.0 / D, [BST(k)], [BST(k + "r")])
                oi = ctr["o"] % 2
                ctr["o"] += 1
                STT(obuf[oi][:, :], xbuf[xi][:, s, :], stat[:, 34 + s:35 + s], fgb[:, :], ALU.mult, ALU.mult,
                    [Bx[xi][s], BST(k + "r"), Bfgb], [Bo[oi]])
                def _store(b=b, st=st, s=s, oi=oi):
                    out_dmas.append(DMA("sp", y_d[b, st * T + s * 128: st * T + (s + 1) * 128, :], obuf[oi][:, :], [Bo[oi]], [], "o%d" % oi))
                deferred_sp.append(_store)

        out_dmas = []
        deferred_sp = []

        def flush_sp():
            while deferred_sp:
                deferred_sp.pop(0)()

        def load_x(xi, ri, b, st):
            DMA("sp", xbuf[xi][:, :, :], x_d[b, st * T:(st + 1) * T, :].rearrange("(s p) d -> p s d", p=128), [], Bx[xi], "x%d" % xi)
            c0 = 16 + st * T
            DMA("sp", ropeb[ri][:, :, :], rope_d[:, :, c0:c0 + T], [], [Brope[ri]], "rope%d" % ri)

        per_tile = [0, 1, 2, 3, 4, 5, 6, WUQ, 7, 8, 9, 13, 11, 15, 10, 14, 12, 16, 17, 18]
        ring_plan([1, 2, 3, 4])
        per_tile_pg = [0, 1, 2, 3, 4, 5, 6, WUQ, 9, 11, 7, 8, 13, 15, 10, 14, 12, 16, 17, 18]
        for b in range(NB):
            for st in range(NST):
                ring_plan(per_tile if (b == 0 and st == 0) else per_tile_pg)

        if stage >= 1:
            DMA("sp", xbuf[0][0:16, 0, :], meta_d, [], Bx[0], "x0")
            DMA("sp", ropeb[0][:, :, 0:16], rope_d[:, :, 0:16], [], [Brope[0]], "rope0")
            phase_norm_T(0, 0, [16], meta=True)
            phase_B(0, 0, 0, 16, meta=True, hook=lambda: run_jit(1))
            phase_C_gate(16, 1, meta=True)
            phase_C_meta()
            DMA("sp", sm_d, S32[:, :, :].rearrange("p h v -> p (h v)"), BS32, [BSm], "smw")
            run_jit(2)

        tiles = [(b, st) for b in range(NB) for st in range(NST)]
        if stage >= 1.2:
            load_x(0, 1, tiles[0][0], tiles[0][1])
        if stage >= 1.5:
            phase_norm_T(0, 0, [128, 128])
        for g, (b, st) in enumerate(tiles):
            if stage < 2:
                break
            xi = g % 2
            ui = g % 2
            ri = (g + 1) % 2
            if st == 0:
                DMA("sp", S32[:, :, :].rearrange("p h v -> p (h v)"), sm_d, [BSm], BS32, "smr")
                for h in range(4):
                    CP("act", S16[0][:, h, :], S32[:, h, :], [BS32[h]], [BS16[0][h]])
            phase_B(ui, ri, 16 + st * T, T, hook=(lambda: run_jit(4)) if g == 0 else None)
            if stage < 3:
                break
            phase_C_gz(ui)
            phase_C_gate(T, 4)
            make_qproj_units(ri)
            if g > 0:
                make_gate_units(ui)
            phase_C_core(ui, st, 0)
            if stage < 4:
                break
            if g + 1 < len(tiles):
                load_x((g + 1) % 2, (g + 2) % 2, tiles[g + 1][0], tiles[g + 1][1])
                phase_norm_T((g + 1) % 2, (g + 1) % 2, [128, 128], part=1)
            phase_D(ui, ri, st)
            if stage < 5:
                break
            if g + 1 < len(tiles):
                phase_norm_T((g + 1) % 2, (g + 1) % 2, [128, 128], part=2)
            phase_E(xi, ui, b, st, pre_gates=(g > 0))
        flush_sp()
        fin = P.add("sp", None)
        for d in out_dmas[-4:]:
            fin.deps.add(d)
        for k, I in P.last_dma.items():
            if str(k).startswith("tap_"):
                fin.deps.add(I)

        cnt, dcnt = P.finalize()
        sbuf_left = nc.sbuf_bytes_remaining
        sems = {}
        for k in P.sem_keys:
            sems[k] = es.enter_context(nc.semaphore("s_%s_%s" % k))
        with nc.Block() as block:
            P.emit(nc, block, sems)
    return nc, (cnt, dcnt, len(P.ins), sbuf_left, P)


def _chunk(Wc):
    return np.ascontiguousarray(Wc.reshape(8, 128, -1).transpose(1, 0, 2).reshape(128, -1))


def _rope_tables():
    inv = (1.0 / (np.float32(10000.0) ** (np.arange(0, 64, 2, dtype=np.float32) / np.float32(64)))).astype(np.float32)
    ang = (np.arange(LTOT, dtype=np.float32)[:, None] * inv[None, :]).astype(np.float32)
    cos = np.cos(ang).astype(np.float32).T
    sin = np.sin(ang).astype(np.float32).T
    cos64 = np.concatenate([cos, cos], 0)
    sin64 = np.concatenate([-sin, sin], 0)
    tab = np.zeros((128, 2, LTOT), np.float32)
    tab[:, 0, :] = np.concatenate([cos64, cos64], 0)
    tab[:, 1, :] = np.concatenate([sin64, sin64], 0)
    return tab


def _consts():
    cf = np.zeros((128, C_W), np.float32)
    cf[:, C_ID:C_ID + 128] = np.eye(128, dtype=np.float32)
    cf[:, C_ONES:C_ONES + 128] = 1.0
    reset = np.ones((128, 256), np.float32)
    reset[:, ::128] = 0.0
    cf[:, C_RESET:C_RESET + 256] = reset
    j = np.arange(128)[:, None]
    i = np.arange(128)[None, :]
    gm = (j <= i).astype(np.float32)
    cf[:, C_GMASK:C_GMASK + 512] = np.tile(gm, (1, 4))
    cf[:, C_TRI:C_TRI + 128] = (j <= i).astype(np.float32)

    return cf


def prep_weights(meta_tokens, norm_g, w_in, gla_gate_w, gla_gate_b, gla_norm_g, gla_proj, mla_q_norm_g, mla_w_uq,
                 mla_kv_norm_g, mla_w_ukv, mla_proj, w_out, final_norm_g):
    f = np.float32
    W = np.asarray(w_in[0], f)
    cuts = np.cumsum([512, 512, 1024, 16, 1024, 256, 128, 64, 1024, 1024, 1024])
    o_q, o_k, o_v, o_lr, o_z, o_cq, o_ckv, o_kr, o_mz, o_gg, o_gm = [0] + list(cuts[:-1])
    kr = W[:, o_kr:o_kr + 64]
    krp = np.concatenate([kr[:, 32:64], kr[:, 0:32]], 1)
    chunks = [W[:, o_q:o_q + 512], W[:, o_k:o_k + 512],
              np.concatenate([W[:, o_cq:o_cq + 256], W[:, o_ckv:o_ckv + 128], kr, kr], 1),
              W[:, o_v:o_v + 512], W[:, o_v + 512:o_v + 1024],
              W[:, o_z:o_z + 512], W[:, o_z + 512:o_z + 1024],
              W[:, o_mz:o_mz + 512], W[:, o_mz + 512:o_mz + 1024],
              W[:, o_gg:o_gg + 512], W[:, o_gg + 512:o_gg + 1024],
              W[:, o_gm:o_gm + 512], W[:, o_gm + 512:o_gm + 1024]]
    for M in (gla_proj, mla_proj, w_out):
        M = np.asarray(M[0], f)
        chunks += [M[:, 0:512], M[:, 512:1024]]
    wst = np.zeros((NCHUNK, 128, 4096), f)
    for c, Wc in enumerate(chunks):
        wst[c] = _chunk(np.ascontiguousarray(Wc))
    Wq = np.asarray(mla_w_uq[0], f).reshape(256, 8, 192)
    nope = Wq[:, :, 0:128].reshape(256, 1024)
    rope = Wq[:, :, 128:192]
    ropep = np.concatenate([rope[:, :, 32:64], rope[:, :, 0:32]], 2)
    Wq_ext = np.concatenate([nope, rope.reshape(256, 512), ropep.reshape(256, 512)], 1)
    wst[WUQ] = np.ascontiguousarray(Wq_ext.reshape(2, 128, 2048).transpose(1, 0, 2).reshape(128, 4096))
    wsm_src = np.concatenate([krp, krp, W[:, o_lr:o_lr + 16]], 1)
    wsm = np.ascontiguousarray(wsm_src.reshape(8, 128, 144).transpose(1, 0, 2).reshape(128, 1152))
    Wkv = np.asarray(mla_w_ukv[0], f).reshape(128, 8, 256)
    wukT = np.ascontiguousarray(Wkv[:, :, 0:128].transpose(2, 1, 0).reshape(128, 1024))
    wuv = np.ascontiguousarray(Wkv[:, :, 128:256].reshape(128, 1024))
    wuk = np.concatenate([wukT, wuv], 1)
    gwb = np.concatenate([np.asarray(gla_gate_w[0], f), np.asarray(gla_gate_b[0], f)[None, :]], 0)
    vecs = np.zeros((128, 16), f)
    vecs[:, 0:8] = np.asarray(norm_g[0], f).reshape(8, 128).T
    vecs[:, 8:10] = np.asarray(gla_norm_g[0], f).reshape(2, 128).T
    vecs[:, 10:12] = np.asarray(mla_q_norm_g[0], f).reshape(2, 128).T
    vecs[:, 12] = np.asarray(mla_kv_norm_g[0], f)
    fgb = np.ascontiguousarray(np.broadcast_to(np.asarray(final_norm_g, f)[None, :], (128, D)))
    return {"meta": np.ascontiguousarray(np.asarray(meta_tokens, f)), "wst": wst, "wsm": wsm, "wuk": wuk, "gwb": gwb,
            "vecs": vecs, "fgb": fgb, "cf32": _consts(), "rope": _rope_tables()}


_CACHE = {}


def kernel(x, meta_tokens, norm_g, w_in, gla_gate_w, gla_gate_b, gla_norm_g, gla_proj, mla_q_norm_g, mla_w_uq,
           mla_kv_norm_g, mla_w_ukv, mla_proj, w_out, final_norm_g):
    ncores = 8
    x = np.asarray(x, np.float32)
    B = x.shape[0]
    NB = B // ncores
    shared = prep_weights(meta_tokens, norm_g, w_in, gla_gate_w, gla_gate_b, gla_norm_g, gla_proj, mla_q_norm_g,
                          mla_w_uq, mla_kv_norm_g, mla_w_ukv, mla_proj, w_out, final_norm_g)
    key = (NB, SEQ // T)
    if key not in _CACHE:
        _CACHE[key] = build(NB, SEQ // T)[0]
    nc = _CACHE[key]
    in_maps = []
    for c in range(ncores):
        m = dict(shared)
        m["x"] = np.ascontiguousarray(x[c * NB:(c + 1) * NB])
        in_maps.append(m)
    res = run_bass_kernel_spmd(nc, in_maps, core_ids=list(range(ncores)))
    return np.concatenate([r["y"] for r in res.results], axis=0)
```

```python
import math
from contextlib import ExitStack

import numpy as np
import concourse.bass as bass
import concourse.mybir as mybir
from concourse.bass_utils import run_bass_kernel_spmd

F32 = mybir.dt.float32
BF16 = mybir.dt.bfloat16
AF = mybir.ActivationFunctionType
ALU = mybir.AluOpType

ENGS = ("pe", "act", "dve", "pool", "sp")


class Buf:
    __slots__ = ("name", "last_w", "readers", "excl")

    def __init__(self, name, excl=False):
        self.name = name
        self.last_w = None
        self.readers = []
        self.excl = excl


class Ins:
    __slots__ = ("eng", "fn", "deps", "needs_inc", "is_dma", "count", "sem_key", "waits", "label")

    def __init__(self, eng, fn, is_dma):
        self.eng = eng
        self.fn = fn
        self.deps = set()
        self.needs_inc = False
        self.is_dma = is_dma
        self.count = None
        self.sem_key = None
        self.waits = None


class Prog:
    def __init__(self):
        self.ins = []
        self.last_dma = {}
        self.label = ""

    def add(self, eng, fn, reads=(), writes=(), dma=None):
        I = Ins(eng, fn, dma is not None)
        I.label = self.label
        if dma is not None:
            I.sem_key = ("dma", dma)
            prev = self.last_dma.get(dma)
            if prev is not None:
                I.deps.add(prev)
            self.last_dma[dma] = I
        ex = [b for b in reads if b.excl]
        if ex:
            reads = [b for b in reads if not b.excl]
            writes = list(writes) + ex
        for b in reads:
            if b.last_w is not None:
                I.deps.add(b.last_w)
        for b in writes:
            if b.last_w is not None:
                I.deps.add(b.last_w)
            for r in b.readers:
                I.deps.add(r)
        for b in reads:
            b.readers.append(I)
        for b in writes:
            b.last_w = I
            b.readers = []
        I.deps.discard(I)
        self.ins.append(I)
        return I

    def finalize(self):
        for I in self.ins:
            if I.eng == "pe" and not I.is_dma:
                I.deps = {d for d in I.deps if not (d.eng == "pe" and not d.is_dma)}
            for d in I.deps:
                d.needs_inc = True
        cnt = {e: 0 for e in ENGS}
        dcnt = {}
        for I in self.ins:
            if I.is_dma:
                dcnt[I.sem_key] = dcnt.get(I.sem_key, 0) + 16
                I.count = dcnt[I.sem_key]
            elif I.needs_inc:
                cnt[I.eng] += 1
                I.sem_key = ("eng", I.eng)
                I.count = cnt[I.eng]
        seen = {e: {} for e in ENGS}
        for I in self.ins:
            w = {}
            for d in I.deps:
                key = d.sem_key
                if w.get(key, 0) < d.count:
                    w[key] = d.count
            s = seen[I.eng]
            out = []
            for key, v in w.items():
                if s.get(key, 0) < v:
                    s[key] = v
                    out.append((key, v))
            I.waits = out
        self.sem_keys = [("eng", e) for e in ENGS] + list(dcnt.keys())
        return cnt, dcnt

    def emit(self, nc, block, sems):
        per = {e: [I for I in self.ins if I.eng == e] for e in ENGS}

        def run(e, eng):
            for I in per[e]:
                for key, v in I.waits:
                    eng.wait_ge(sems[key], v)
                if I.fn is None:
                    continue
                r = I.fn(eng)
                if I.is_dma:
                    r.then_inc(sems[I.sem_key], 16)
                elif I.needs_inc:
                    r.then_inc(sems[I.sem_key], 1)

        @block.tensor
        def _(eng):
            run("pe", eng)

        @block.scalar
        def _(eng):
            run("act", eng)

        @block.vector
        def _(eng):
            run("dve", eng)

        @block.gpsimd
        def _(eng):
            run("pool", eng)

        @block.sync
        def _(eng):
            run("sp", eng)


D = 1024
SEQ = 2048
NMETA = 16
LTOT = SEQ + NMETA
T = 256
NSUB = 2
NCHUNK = 20
WUQ = 19
NRING = 4
EPS = 1e-6
ATT_SCALE = 1.0 / math.sqrt(192.0)
QSCALE = 128.0 ** -0.5

C_ID = 0
C_ONES = 128
C_RESET = 256
C_GMASK = 512
C_TRI = 1024
C_W = 1152


def build(NB, NST, taps=(), stage=9):
    nc = bass.Bass("TRN2", target_bir_lowering=False)
    P = Prog()
    dram_in = lambda n, s: nc.dram_tensor(n, s, F32, kind="ExternalInput").ap()
    x_d = dram_in("x", [NB, SEQ, D])
    meta_d = dram_in("meta", [NMETA, D])
    wst_d = dram_in("wst", [NCHUNK, 128, 4096])
    wsm_d = dram_in("wsm", [128, 1152])
    wuk_d = dram_in("wuk", [128, 2048])
    gwb_d = dram_in("gwb", [17, 512])
    vecs_d = dram_in("vecs", [128, 16])
    fgb_d = dram_in("fgb", [128, D])
    cf_d = dram_in("cf32", [128, C_W])
    rope_d = dram_in("rope", [128, 2, LTOT])
    y_d = nc.dram_tensor("y", [NB, SEQ, D], F32, kind="ExternalOutput").ap()
    wbf_d = nc.dram_tensor("wbf", [NCHUNK, 128, 4096], BF16, kind="Internal").ap()
    sm_d = nc.dram_tensor("smeta", [128, 1024], F32, kind="Internal").ap()
    tap_d = {}
    for name, shape in taps:
        tap_d[name] = nc.dram_tensor("tap_" + name, list(shape), F32, kind="ExternalOutput").ap()

    with ExitStack() as es:
        def sb(name, shape, dt):
            return es.enter_context(nc.sbuf_tensor("sb_" + name, shape, dt))

        def ps(name, shape, dt):
            return es.enter_context(nc.psum_tensor("ps_" + name, shape, dt))

        cf = sb("cf", [128, C_W], F32); Bcf = Buf("cf")
        identb = sb("identb", [128, 128], BF16); Bid = Buf("identb")
        trib = sb("trib", [128, 128], BF16); Btri = Buf("trib")
        vecs = sb("vecs", [128, 16], F32); Bvecs = Buf("vecs")
        fgb = sb("fgb", [128, D], F32); Bfgb = Buf("fgb")
        gwb = sb("gwb", [17, 512], BF16); Bgwb = Buf("gwb")
        wsm = sb("wsm", [128, 8, 144], BF16); Bwsm = Buf("wsm")
        wukT = sb("wukT", [128, 8, 128], BF16); Bwuk = Buf("wukT")
        wuv = sb("wuv", [128, 8, 128], BF16); Bwuv = Buf("wuv")
        ckvnT = sb("ckvnT", [128, LTOT], BF16)
        kropeT = sb("kropeT", [128, LTOT], BF16)
        ckvtok = sb("ckvtok", [128, 17, 130], BF16)
        Bcache = [Buf("cache%d" % j) for j in range(17)]
        S32 = sb("S32", [128, 4, 256], F32)
        S16 = [sb("S16_%d" % i, [128, 4, 256], BF16) for i in range(2)]
        BS32 = [Buf("S32_%d" % h) for h in range(4)]
        BS16 = [[Buf("S16_%d_%d" % (i, h)) for h in range(4)] for i in range(2)]
        BSm = Buf("Sm32")
        ring = [sb("ring%d" % i, [128, 4096], BF16) for i in range(NRING)]
        Bring = [Buf("ring%d" % i) for i in range(NRING)]
        Bscr = [Buf("scr%d" % c) for c in range(NCHUNK)]

        xbuf = [sb("xbuf%d" % i, [128, 2, D], F32) for i in range(2)]
        Bx = [[Buf("x%d_%d" % (i, s)) for s in range(2)] for i in range(2)]
        ropeb = [sb("ropeb%d" % i, [128, 2, T], F32) for i in range(2)]
        Brope = [Buf("rope%d" % i) for i in range(2)]
        junk = sb("junk", [128, D], BF16); Bjunk = Buf("junk")
        reset4 = sb("reset4", [128, 1024], BF16); Breset4 = Buf("reset4")
        stat = sb("stat", [128, 64], F32)
        mhalf = sb("mhalf", [128, 8], F32); Bmh = Buf("mhalf")
        Bst = {}

        def BST(k):
            if k not in Bst:
                Bst[k] = Buf("stat_" + k)
            return Bst[k]
        xn = sb("xn", [128, 2, D], BF16); Bxn = [Buf("xn0"), Buf("xn1")]
        uT = [sb("uT%d" % i, [128, 8, T], BF16) for i in range(2)]
        BuT = [[Buf("uT%d_%d" % (i, s)) for s in range(2)] for i in range(2)]
        uTm = sb("uTm", [128, 8, 16], BF16); BuTm = Buf("uTm")
        qT = sb("qT", [128, 4, T], F32); BqT = [Buf("qT%d" % h) for h in range(4)]
        kT = sb("kT", [128, 4, T], F32); BkT = [Buf("kT%d" % h) for h in range(4)]
        glrT = sb("glrT", [17, T], BF16); Bglr = Buf("glrT")
        vtok = sb("vtok", [128, 2, D], BF16); Bv = [[Buf("v%d_%d" % (s, hf)) for hf in range(2)] for s in range(2)]
        cqT = sb("cqT", [128, 2, T], F32); BcqT = [Buf("cqT0"), Buf("cqT1")]
        cqsq = sb("cqsq", [128, 2, T], F32); Bcqsq = [Buf("cqsq0"), Buf("cqsq1")]
        ckvT = sb("ckvT", [128, T], F32); BckvT = Buf("ckvT")
        ckvsq = sb("ckvsq", [128, T], F32); Bckvsq = Buf("ckvsq")
        rt = [sb("rt%d" % i, [128, T], F32) for i in range(2)]; Brt = [Buf("rt%d" % i) for i in range(2)]
        rt2 = [sb("rt2_%d" % i, [128, 2 * T], F32) for i in range(2)]; Brt2 = [Buf("rt2_%d" % i) for i in range(2)]
        Gb = sb("Gb", [128, 4, T], F32); BGb = [Buf("Gb%d" % h) for h in range(4)]
        BNb = sb("BNb", [128, 4, T], F32); BBN = [Buf("BN%d" % h) for h in range(4)]
        dec = sb("dec", [128, 4, 4], F32); Bdec = [Buf("dec%d" % h) for h in range(4)]
        qeT = sb("qeT", [128, 4, T], BF16); BqeT = [Buf("qeT%d" % h) for h in range(4)]
        keT = sb("keT", [128, 4, T], BF16); BkeT = [Buf("keT%d" % h) for h in range(4)]
        klT = sb("klT", [128, 4, T], BF16); BklT = [Buf("klT%d" % h) for h in range(4)]
        kltok = sb("kltok", [128, 2, 512], BF16); Bkltok = [Buf("kltok0"), Buf("kltok1")]
        ATm = [sb("ATm%d" % i, [128, 512], BF16) for i in range(2)]; BATm = [Buf("ATm0"), Buf("ATm1")]
        siluz = sb("siluz", [128, 2, D], BF16); Bsz = [[Buf("sz%d_%d" % (s, hf)) for hf in range(2)] for s in range(2)]
        Xa = sb("Xa", [128, 2, D], BF16); BXa = [[Buf("Xa%d_%d" % (s, h)) for h in range(4)] for s in range(2)]
        XaT = sb("XaT", [128, 8, T], BF16); BXaT = [Buf("XaT0"), Buf("XaT1")]
        rstdq = sb("rstdq", [128, T], F32); Brq = Buf("rstdq")
        rstdkv = sb("rstdkv", [128, T], F32); Brkv = Buf("rstdkv")
        cqnT = sb("cqnT", [128, 2, T], BF16); Bcqn = [Buf("cqn0"), Buf("cqn1")]
        qnP = [sb("qnP%d" % i, [128, 2 * T], BF16) for i in range(2)]; BqnP = [Buf("qnP0"), Buf("qnP1")]
        qabsT = sb("qabsT", [128, 8, T], BF16); Bqabs = [Buf("qabs%d" % h) for h in range(8)]
        qropeZ = sb("qropeZ", [128, 8, T], BF16); Bqrope = [Buf("qrope%d" % h) for h in range(8)]
        ckvnTm = sb("ckvnTm", [128, 128], BF16)
        kropeTm = sb("kropeTm", [128, 128], BF16)
        pT = [sb("pT%d" % i, [128, 512], BF16) for i in range(4)]; BpT = [Buf("pT%d" % i) for i in range(4)]
        olat = Xa[:, :, :].rearrange("p s (h c) -> p s h c", h=8)
        Bolat = [[BXa[s][h // 2] for h in range(8)] for s in range(2)]
        olatT = sb("olatT", [128, 8, T], BF16); BolatT = [Buf("olatT%d" % h) for h in range(8)]
        dgf = olatT[:, 0:4, :].rearrange("p h t -> p (h t)")
        siluT = [sb("siluT%d" % i, [128, T], F32) for i in range(2)]; BsiluT = [Buf("siluT0"), Buf("siluT1")]
        XbT = sb("XbT", [128, 8, T], BF16); BXbT = [Buf("XbT%d" % h) for h in range(8)]
        sa = sb("sa", [128, 4, T], F32); Bsa = [Buf("sa%d" % c) for c in range(4)]
        sbg = sb("sbg", [128, 4, T], F32); Bsb = [Buf("sb%d" % c) for c in range(4)]
        mT = sb("mT", [128, 8, T], BF16); BmT = [Buf("mT%d" % c) for c in range(8)]
        obuf = [sb("obuf%d" % i, [128, D], F32) for i in range(2)]; Bo = [Buf("o0"), Buf("o1")]

        PB = [ps("PB%d" % i, [128, 512], F32) for i in range(8)]
        BPB = [Buf("PB%d" % i, True) for i in range(8)]
        R = PB[0:4]; BR = BPB[0:4]
        A = PB[4:6]; BA = BPB[4:6]
        TB = [PB[6][:, :].bitcast(BF16), PB[7][:, :].bitcast(BF16)]; BTB = BPB[6:8]
        A4 = [PB[4], PB[5], PB[6], PB[7]]; BA4 = BPB[4:8]
        ctr = {"r": 0, "tb": 0, "a": 0, "a4": 0, "pt": 0, "alt": 0, "o": 0, "u": 0}

        reserved = set()

        def nextR():
            while True:
                i = ctr["r"] % 4
                ctr["r"] += 1
                if i not in reserved:
                    return R[i], BR[i]

        resv8 = set()

        def nextU():
            while True:
                i = ctr["u"] % 8
                ctr["u"] += 1
                if i not in resv8:
                    return PB[i], BPB[i]

        def nextRi():
            while True:
                i = ctr["r"] % 4
                ctr["r"] += 1
                if i not in reserved:
                    return i

        def nextTB():
            i = ctr["tb"] % 2
            ctr["tb"] += 1
            return TB[i], BTB[i]

        def nextA():
            i = ctr["a"] % 2
            ctr["a"] += 1
            return A[i], BA[i]

        def nextA4():
            i = ctr["a4"] % 4
            ctr["a4"] += 1
            return A4[i], BA4[i]

        def nextPT():
            i = ctr["pt"] % 4
            ctr["pt"] += 1
            return pT[i], BpT[i]

        def MM(out, lhsT, rhs, start, stop, reads, writes, sgc=False):
            return P.add("pe", lambda e: e.matmul(out, lhsT=lhsT, rhs=rhs, start=start, stop=stop, skip_group_check=sgc), reads, writes)

        def TR(out, in_, k, reads, writes):
            return P.add("pe", lambda e: e.transpose(out=out, in_=in_, identity=identb[0:k, 0:k]), list(reads) + [Bid], writes)

        def ACT(out, in_, func, reads, writes, bias=None, scale=None, accum=None):
            kw = {}
            if bias is not None:
                kw["bias"] = bias
            if scale is not None:
                kw["scale"] = scale
            if accum is not None:
                kw["accum_out"] = accum
            return P.add("act", lambda e: e.activation(out=out, in_=in_, func=func, **kw), reads, writes)

        def TT(out, in0, in1, op, reads, writes, eng="dve"):
            return P.add(eng, lambda e: e.tensor_tensor(out=out, in0=in0, in1=in1, op=op), reads, writes)

        def TS(out, in0, s1, reads, writes, op0=ALU.mult, s2=None, op1=None, eng="dve"):
            if op1 is None:
                return P.add(eng, lambda e: e.tensor_scalar(out=out, in0=in0, scalar1=s1, scalar2=None, op0=op0), reads, writes)
            return P.add(eng, lambda e: e.tensor_scalar(out=out, in0=in0, scalar1=s1, scalar2=s2, op0=op0, op1=op1), reads, writes)

        def STT(out, in0, scalar, in1, op0, op1, reads, writes):
            return P.add("dve", lambda e: e.scalar_tensor_tensor(out=out, in0=in0, scalar=scalar, in1=in1, op0=op0, op1=op1), reads, writes)

        def CP(eng, out, in_, reads, writes):
            if eng == "act":
                return P.add("act", lambda e: e.activation(out=out, in_=in_, func=AF.Copy), reads, writes)
            return P.add(eng, lambda e: e.tensor_copy(out=out, in_=in_), reads, writes)

        def alt():
            ctr["alt"] += 1
            return "act" if ctr["alt"] % 2 else "dve"

        def DMA(q, out, in_, reads, writes, key):
            return P.add(q, lambda e: e.dma_start(out=out, in_=in_), reads, writes, dma=key)

        def tap(name, src, reads, rows=128):
            if name in tap_d:
                DMA("sp", tap_d[name], src, reads, [], "tap_" + name)

        def rstd_pow(out, in_, k, inv_n, reads, writes):
            rows = out.shape[0]
            TS(out, in_, inv_n, reads, writes, op0=ALU.mult, s2=EPS, op1=ALU.add)
            TT(out, out, mhalf[0:rows, 0:k], ALU.pow, list(writes) + [Bmh], writes, eng="pool")

        def rstd_chain(out, in_, key, inv_n, reads, writes):
            ACT(out, in_, AF.Ln, reads, writes, bias=EPS, scale=inv_n)
            ACT(out, out, AF.Exp, writes, writes, scale=-0.5)

        ring_state = {"pos": 0, "issued": 0, "seq": []}

        def ring_plan(seq):
            ring_state["seq"].extend(seq)

        def ring_issue_upto(p):
            while ring_state["issued"] <= p and ring_state["issued"] < len(ring_state["seq"]):
                i = ring_state["issued"]
                c = ring_state["seq"][i]
                slot = i % NRING
                DMA("sp", ring[slot][:, :], wbf_d[c], [Bscr[c]], [Bring[slot]], "ring%d" % slot)
                ring_state["issued"] += 1

        def ring_acquire(c):
            p = ring_state["pos"]
            while stage < 9 and ring_state["seq"][p] != c:
                ring_state["seq"].pop(p)
            assert ring_state["seq"][p] == c, (p, c, ring_state["seq"][p])
            ring_issue_upto(p + NRING - 1)
            ring_state["pos"] += 1
            slot = p % NRING
            return ring[slot], Bring[slot]

        DMA("sp", cf[:, :], cf_d, [], [Bcf], "cf")
        DMA("sp", vecs[:, :], vecs_d, [], [Bvecs], "vecs")
        DMA("sp", fgb[:, :], fgb_d, [], [Bfgb], "fgb")
        CP("dve", identb[:, :], cf[:, C_ID:C_ID + 128], [Bcf], [Bid])
        TS(trib[:, :], cf[:, C_TRI:C_TRI + 128], -1.0, [Bcf], [Btri], op0=ALU.add, s2=30000.0, op1=ALU.mult)
        P.add("dve", lambda e: e.memset(ckvtok[:, :, 128:130], 1.0), [], Bcache)
        P.add("dve", lambda e: e.memset(ckvtok[:, 0, :], 0.0), [], [Bcache[0]])
        P.add("dve", lambda e: e.memset(ckvtok[0:16, 0, 128:130], 1.0), [], [Bcache[0]])
        P.add("dve", lambda e: e.memset(ckvnTm[:, :], 0.0), [], [Bcache[0]])
        P.add("dve", lambda e: e.memset(kropeTm[:, :], 0.0), [], [Bcache[0]])
        P.add("dve", lambda e: e.memset(qropeZ[:, :, :], 0.0), [], Bqrope)
        P.add("dve", lambda e: e.memset(glrT[:, :], 1.0), [], [Bglr])
        P.add("dve", lambda e: e.memset(mhalf[:, :], -0.5), [], [Bmh])
        P.add("dve", lambda e: e.memset(reset4[:, :], 1.0), [], [Breset4])
        P.add("dve", lambda e: e.memset(reset4[:, :].rearrange("p (c t) -> p c t", t=128)[:, :, 0:1], 0.0), [], [Breset4])
        xflat = [xbuf[i][:, :, :].rearrange("p s d -> p (s d)") for i in range(2)]
        Bxall = [Bx[i] for i in range(2)]
        DMA("sp", xflat[1][:, 0:1152], wsm_d, [], Bxall[1], "x1")
        for kc in range(8):
            TS(wsm[:, kc, :], xflat[1][:, kc * 144:(kc + 1) * 144], vecs[:, kc:kc + 1], Bxall[1] + [Bvecs], [Bwsm])
        DMA("sp", xflat[1][:, 0:2048], wuk_d, [Bwsm], Bxall[1], "x1")
        CP("dve", wukT[:, :, :].rearrange("p h c -> p (h c)"), xflat[1][:, 0:1024], Bxall[1], [Bwuk])
        TS(wuv[:, :, :].rearrange("p h c -> p (h c)"), xflat[1][:, 1024:2048], vecs[:, 12:13], Bxall[1] + [Bvecs], [Bwuv])
        DMA("sp", xflat[1][0:17, 0:512], gwb_d, [Bwuk, Bwuv], Bxall[1], "x1")
        CP("dve", gwb[:, :], xflat[1][0:17, 0:512], Bxall[1], [Bgwb])
        Bpiece = [[Buf("piece%d_%d" % (sl, k)) for k in range(8)] for sl in range(NRING)]
        cast_engs = ["dve", "act", "dve"]
        ce = 0

        def flat4(t):
            return t[:, :, :].rearrange("p a b -> p (a b)")
        stg = [(xflat[0][:, 0:1024], [Bx[0][0]]), (xflat[0][:, 1024:2048], [Bx[0][1]]),
               (xflat[1][:, 0:1024], [Bx[1][0]]), (xflat[1][:, 1024:2048], [Bx[1][1]]),
               (flat4(qT), BqT), (flat4(kT), BkT), (flat4(Gb), BGb), (flat4(BNb), BBN),
               (flat4(sa), Bsa), (flat4(sbg), Bsb), (flat4(S32), BS32)]
        NSTG = len(stg)

        GROUP_A = [1, 2, 3, 4, 0, 5, 6, WUQ, 7, 8]
        GROUP_B = [9, 13, 11, 15, 10, 14, 12, 16, 17, 18]

        def gain_ap(c, q, kc):
            if c == WUQ:
                return vecs[:, 10 + q // 2:11 + q // 2]
            if c <= 12:
                return vecs[:, kc:kc + 1]
            if c <= 14:
                return vecs[:, 8 + (kc % 2):9 + (kc % 2)]
            return None

        def cast_piece(eng, dst, src, sc, rd, wr):
            if sc is None:
                CP(eng, dst, src, rd, wr)
            elif eng == "act":
                ACT(dst, src, AF.Copy, list(rd) + [Bvecs], wr, scale=sc)
            else:
                TS(dst, src, sc, list(rd) + [Bvecs], wr, eng=eng)

        def q_load(i):
            c, q = GROUP_A[i // 4], i % 4
            ap, bufs = stg[i % NSTG]
            DMA("sp", ap, wst_d[c, :, q * 1024:(q + 1) * 1024], [], bufs, "stg%d" % (i % NSTG))

        def q_cast(i):
            nonlocal ce
            ci, q = divmod(i, 4)
            c = GROUP_A[ci]
            slot = ci % NRING
            ap, bufs = stg[i % NSTG]
            for k2 in range(2):
                kc = 2 * q + k2
                eng = cast_engs[ce % 3]
                ce += 1
                cast_piece(eng, ring[slot][:, kc * 512:(kc + 1) * 512], ap[:, k2 * 512:(k2 + 1) * 512], gain_ap(c, q, kc),
                           bufs, [Bpiece[slot][kc]])
            if q == 3:
                DMA("sp", wbf_d[c], ring[slot][:, :], Bpiece[slot] + [Bring[slot]], [Bscr[c]], "ringw%d" % slot)

        NQ = 4 * len(GROUP_A)
        LOOK = NSTG - 1
        for i in range(NQ + LOOK):
            if i < NQ:
                q_load(i)
            if i >= LOOK:
                q_cast(i - LOOK)

        stgB = [(flat4(sa), Bsa), (flat4(sbg), Bsb), (obuf[0][:, :], [Bo[0]]), (obuf[1][:, :], [Bo[1]])]
        bfst = [(mT[:, :, :].rearrange("p a b -> p (a b)"), BmT), (XbT[:, :, :].rearrange("p a b -> p (a b)"), BXbT)]
        NJ = 4 * len(GROUP_B)
        jit_units = []

        def j_load(j):
            c, q = GROUP_B[j // 4], j % 4
            ap, bufs = stgB[j % 4]
            DMA("sp", ap, wst_d[c, :, q * 1024:(q + 1) * 1024], [], bufs, "jl%d" % (j % 4))

        def j_unit(j):
            def f():
                nonlocal ce
                if j + 3 < NJ:
                    j_load(j + 3)
                c, q = GROUP_B[j // 4], j % 4
                ap, bufs = stgB[j % 4]
                hh = j // 2
                bap, bbufs = bfst[hh % 2]
                for k2 in range(2):
                    kc = 2 * q + k2
                    eng = cast_engs[ce % 3]
                    ce += 1
                    o = ((q % 2) * 2 + k2) * 512
                    cast_piece(eng, bap[:, o:o + 512], ap[:, k2 * 512:(k2 + 1) * 512], gain_ap(c, q, kc), bufs, bbufs)
                if q % 2 == 1:
                    DMA("sp", wbf_d[c, :, (q // 2) * 2048:(q // 2 + 1) * 2048], bap, bbufs, [Bscr[c]], "js%d" % (hh % 2))
            return f

        for j in range(3):
            j_load(j)
        for j in range(NJ):
            jit_units.append(j_unit(j))

        def run_jit(k):
            lab = P.label
            P.label = "JIT"
            while k > 0 and jit_units:
                jit_units.pop(0)()
                k -= 1
            P.label = lab

        def phase_norm_T(xi, ui, n_rows_list, meta=False, part=3):
            P.label = "A"
            for s, rows in enumerate(n_rows_list):
                xs = xbuf[xi][0:rows, s, :]
                k = "a%d" % s
                if part & 1:
                    ACT(junk[0:rows, :], xs, AF.Square, [Bx[xi][s]], [Bjunk, BST(k)], accum=stat[0:rows, s:s + 1])
                    rstd_pow(stat[0:rows, 2 + s:3 + s], stat[0:rows, s:s + 1], 1, 1.0 / D, [BST(k)], [BST(k + "r")])
                    TS(xn[0:rows, s, :], xs, stat[0:rows, 2 + s:3 + s], [Bx[xi][s], BST(k + "r")], [Bxn[s]])
                if part & 2:
                    tb, Btb = nextTB()
                    for c in range(8):
                        TR(tb[:, c * rows:(c + 1) * rows], xn[0:rows, s, c * 128:(c + 1) * 128], rows, [Bxn[s]], [Btb])
                    src = tb[:, 0:8 * rows].rearrange("p (c t) -> p c t", c=8)
                    if meta:
                        CP(alt(), uTm[:, :, :], src, [Btb], [BuTm])
                    else:
                        CP(alt(), uT[ui][:, :, s * 128:(s + 1) * 128], src, [Btb], [BuT[ui][s]])

        def proj_fm(wtile, Bw, col0, m, u_ap, Bu, n, ncol=512):
            r, Br = nextR()
            for kc in range(8):
                MM(r[0:m, 0:n], wtile[:, kc * ncol + col0: kc * ncol + col0 + m], u_ap[:, kc, 0:n], kc == 0, kc == 7,
                   [Bw] + Bu, [Br])
            return r, Br

        def gating_all(n, nch, meta):
            cl = n // nch
            if meta:
                for h in range(4):
                    r, Br = nextR()
                    MM(r[:, 0:n], gwb[0:17, h * 128:(h + 1) * 128], glrT[0:17, 0:n], True, True, [Bgwb, Bglr], [Br])
                    ACT(Gb[:, h, 0:n], r[:, 0:n], AF.Exp, [Br], [BGb[h]], scale=-1.0)
                    ACT(Gb[:, h, 0:n], Gb[:, h, 0:n], AF.Ln, [BGb[h]], [BGb[h]], bias=1.0)
                    P.add("dve", lambda e, h=h: e.tensor_tensor_scan(out=BNb[:, h, 0:n], data0=cf[:, C_RESET:C_RESET + n],
                                                                     data1=Gb[:, h, 0:n], initial=0.0, op0=ALU.mult, op1=ALU.add),
                          [Bcf, BGb[h]], [BBN[h]])
                    ACT(Gb[:, h, 0:n], BNb[:, h, 0:n], AF.Exp, [BBN[h]], [BGb[h]], scale=-1.0 / 16)
                    ACT(BNb[:, h, 0:n], BNb[:, h, 0:n], AF.Exp, [BBN[h]], [BBN[h]], scale=1.0 / 16)
                    src = Gb[:, h, 0:n].rearrange("p (c t) -> p c t", t=cl)[:, :, cl - 1:cl]
                    CP("act", dec[:, h, 0:nch].rearrange("p (c o) -> p c o", o=1), src, [BGb[h]], [Bdec[h]])
                return
            Gf = Gb[:, :, :].rearrange("p h t -> p (h t)")
            Bf = BNb[:, :, :].rearrange("p h t -> p (h t)")
            for hp in range(2):
                r, Br = nextR()
                for u in range(2):
                    h = 2 * hp + u
                    MM(r[:, u * T:(u + 1) * T], gwb[0:17, h * 128:(h + 1) * 128], glrT[0:17, 0:T], True, True, [Bgwb, Bglr], [Br])
                ACT(Gf[:, hp * 512:(hp + 1) * 512], r[:, :], AF.Exp, [Br], [BGb[2 * hp], BGb[2 * hp + 1]], scale=-1.0)
            ACT(Gf, Gf, AF.Ln, BGb, BGb, bias=1.0)
            P.add("dve", lambda e: e.tensor_tensor_scan(out=Bf, data0=reset4[:, :], data1=Gf, initial=0.0,
                                                        op0=ALU.mult, op1=ALU.add), [Breset4] + BGb, BBN)
            ACT(Gf, Bf, AF.Exp, BBN, BGb, scale=-1.0 / 16)
            ACT(Bf, Bf, AF.Exp, BBN, BBN, scale=1.0 / 16)
            src = Gf.rearrange("p (h c t) -> p h c t", h=4, c=2)[:, :, :, cl - 1:cl]
            CP("act", dec[:, :, 0:2].rearrange("p h (c o) -> p h c o", o=1), src, BGb, Bdec)

        def gating_part2(n, nch, meta):
            cl = n // nch
            P.label = "B.gate2"
            for h in range(4):
                if not meta:
                    TT(qeT[:, h, 0:n], qT[:, h, 0:n], Gb[:, h, 0:n], ALU.mult, [BqT[h], BGb[h]], [BqeT[h]], eng="pool")
                    TT(keT[:, h, 0:n], kT[:, h, 0:n], BNb[:, h, 0:n], ALU.mult, [BkT[h], BBN[h]], [BkeT[h]], eng="pool")
                for c in range(nch):
                    STT(klT[:, h, c * cl:(c + 1) * cl], kT[:, h, c * cl:(c + 1) * cl], dec[:, h, c:c + 1],
                        BNb[:, h, c * cl:(c + 1) * cl], ALU.mult, ALU.mult, [BkT[h], Bdec[h], BBN[h]], [BklT[h]])

        def phase_B(ui, ri, c0, n, meta=False, hook=None):
            u_ap = uTm if meta else uT[ui]
            Bu = [BuTm] if meta else BuT[ui]
            wsmf = wsm[:, :, :].rearrange("p k c -> p (k c)")
            P.label = "B.small"
            r, Br = proj_fm(wsmf, Bwsm, 128, 16, u_ap, Bu, n, ncol=144)
            CP("act", glrT[0:16, 0:n], r[0:16, 0:n], [Br], [Bglr])
            P.label = "B.q"
            if not meta:
                w, Bw = ring_acquire(0)
                for h in range(4):
                    r, Br = proj_fm(w, Bw, h * 128, 128, u_ap, Bu, n)
                    ACT(qT[:, h, 0:n], r[:, 0:n], AF.Copy, [Br], [BqT[h]], scale=QSCALE)
            if hook: hook()
            P.label = "B.k"
            w, Bw = ring_acquire(1)
            for h in range(4):
                r, Br = proj_fm(w, Bw, h * 128, 128, u_ap, Bu, n)
                CP("dve", kT[:, h, 0:n], r[:, 0:n], [Br], [BkT[h]])
            P.label = "B.small"
            r, Br = proj_fm(wsmf, Bwsm, 0, 128, u_ap, Bu, n, ncol=144)
            TT(rt[1][:, 0:n], r[:, 0:n], ropeb[ri][:, 1, 0:n], ALU.mult, [Br, Brope[ri]], [Brt[1]])
            if hook: hook()
            P.label = "B.c2"
            w, Bw = ring_acquire(2)
            r, Br = proj_fm(w, Bw, 256, 128, u_ap, Bu, n)
            CP("dve", ckvT[:, 0:n], r[:, 0:n], [Br], [BckvT])
            ACT(ckvsq[:, 0:n], r[:, 0:n], AF.Square, [Br], [Bckvsq])
            if not meta:
                for k2 in range(2):
                    r, Br = proj_fm(w, Bw, k2 * 128, 128, u_ap, Bu, n)
                    CP("dve", cqT[:, k2, 0:n], r[:, 0:n], [Br], [BcqT[k2]])
                    ACT(cqsq[:, k2, 0:n], r[:, 0:n], AF.Square, [Br], [Bcqsq[k2]])
            r, Br = proj_fm(w, Bw, 384, 128, u_ap, Bu, n)
            TT(rt[0][:, 0:n], r[:, 0:n], ropeb[ri][:, 0, 0:n], ALU.mult, [Br, Brope[ri]], [Brt[0]])
            if meta:
                TT(kropeTm[:, 0:n], rt[0][:, 0:n], rt[1][:, 0:n], ALU.add, [Brt[0], Brt[1]], [Bcache[0]])
            else:
                kb = [Bcache[1 + (c0 - 16) // 128], Bcache[2 + (c0 - 16) // 128]]
                TT(kropeT[:, c0:c0 + n], rt[0][:, 0:n], rt[1][:, 0:n], ALU.add, [Brt[0], Brt[1]], kb, eng="pool")
            P.label = "B.mla"
            mla_norm_cache(c0, n, meta)
            if not meta:
                r, Br = nextR()
                for k2 in range(2):
                    MM(r[:, 0:n], cf[:, C_ONES:C_ONES + 128], cqsq[:, k2, 0:n], k2 == 0, k2 == 1, [Bcf, Bcqsq[k2]], [Br])
                rstd_chain(rstdq[:, 0:n], r[:, 0:n], "q", 1.0 / 256, [Br], [Brq])
                for k2 in range(2):
                    TT(cqnT[:, k2, :], cqT[:, k2, :], rstdq[:, :], ALU.mult, [BcqT[k2], Brq], [Bcqn[k2]])
            if hook: hook()
            P.label = "B.small"
            gating_all(n, 1 if meta else 2, meta)
            P.label = "B.v"
            for hf in range(2):
                w, Bw = ring_acquire(3 + hf)
                subs = [(0, 16)] if meta else [(0, 128), (1, 128)]
                for (s, rows) in subs:
                    r, Br = nextR()
                    for kc in range(8):
                        MM(r[0:rows, :], u_ap[:, kc, s * 128:s * 128 + rows], w[:, kc * 512:(kc + 1) * 512], kc == 0, kc == 7,
                           [Bw] + Bu, [Br])
                    CP(alt(), vtok[0:rows, s, hf * 512:(hf + 1) * 512], r[0:rows, :], [Br], [Bv[s][hf]])
            if hook: hook()
            P.label = "B.mla"
            mla_cache_tok(c0, n, meta)
            gating_part2(n, 1 if meta else 2, meta)

        def phase_C_gate(n, nch, meta=False):
            P.label = "C.gate"
            subs = [(0, 16)] if meta else [(0, 128), (1, 128)]
            for (s, rows) in subs:
                tb, Btb = nextTB()
                for h in range(4):
                    TR(tb[0:rows, h * 128:(h + 1) * 128], klT[:, h, s * 128:s * 128 + rows], 128, [BklT[h]], [Btb])
                CP(alt(), kltok[0:rows, s, :], tb[0:rows, 0:512], [Btb], [Bkltok[s]])

        def phase_C_meta():
            for h in range(4):
                a, Ba = nextA()
                MM(a[:, 0:256], kltok[0:16, 0, h * 128:(h + 1) * 128], vtok[0:16, 0, h * 256:(h + 1) * 256], True, True,
                   [Bkltok[0], Bv[0][h // 2]], [Ba])
                CP("dve", S32[:, h, :], a[:, 0:256], [Ba], [BS32[h]])

        def phase_C_gz(ui):
            P.label = "C.gz"
            for hf in range(2):
                w, Bw = ring_acquire(5 + hf)
                for s in range(2):
                    r, Br = nextR()
                    for kc in range(8):
                        MM(r[:, :], uT[ui][:, kc, s * 128:(s + 1) * 128], w[:, kc * 512:(kc + 1) * 512], kc == 0, kc == 7,
                           [Bw] + BuT[ui], [Br])
                    ACT(siluz[:, s, hf * 512:(hf + 1) * 512], r[:, :], AF.Silu, [Br], [Bsz[s][hf]])

        def phase_C_core(ui, st, chunk_par):
            for s in range(2):
                P.label = "C.core"
                r, Br = nextR()
                for h in range(4):
                    MM(r[:, h * 128:(h + 1) * 128], keT[:, h, s * 128:(s + 1) * 128], qeT[:, h, s * 128:(s + 1) * 128], True, True,
                       [BkeT[h], BqeT[h]], [Br])
                am, Bam = ATm[s], BATm[s]
                TT(am[:, :], r[:, :], cf[:, C_GMASK:C_GMASK + 512], ALU.mult, [Br, Bcf], [Bam])
                run_units(1)
                pa = (chunk_par + s) % 2
                pb = (pa + 1) % 2
                oi = [nextRi(), nextRi()]
                reserved.update(oi)
                resv8.update(oi)
                obank = [(R[oi[h // 2]][:, (h % 2) * 256:(h % 2) * 256 + 256], BR[oi[h // 2]]) for h in range(4)]
                for h in range(4):
                    r, Br = obank[h]
                    MM(r[:, :], am[:, h * 128:(h + 1) * 128], vtok[:, s, h * 256:(h + 1) * 256], h % 2 == 0, False,
                       [Bam, Bv[s][h // 2]], [Br], sgc=True)
                for h in range(4):
                    r, Br = obank[h]
                    MM(r[:, :], qeT[:, h, s * 128:(s + 1) * 128], S16[pa][:, h, :], False, True,
                       [BqeT[h], BS16[pa][h]], [Br], sgc=True)
                for h in range(4):
                    r, Br = obank[h]
                    ACT(junk[:, 0:256], r[:, :], AF.Square, [Br], [Bjunk, BST("oss%d" % h)], accum=stat[:, 8 + h:9 + h])
                for h in range(4):
                    a, Ba = nextA4()
                    MM(a[:, 0:256], kltok[:, s, h * 128:(h + 1) * 128], vtok[:, s, h * 256:(h + 1) * 256], True, True,
                       [Bkltok[s], Bv[s][h // 2]], [Ba])
                    STT(S32[:, h, :], S32[:, h, :], dec[:, h, s:s + 1], a[:, 0:256], ALU.mult, ALU.add,
                        [BS32[h], Bdec[h], Ba], [BS32[h]])
                    CP("act", S16[pb][:, h, :], S32[:, h, :], [BS32[h]], [BS16[pb][h]])
                for hp in range(2):
                    TT(Xa[:, s, hp * 512:(hp + 1) * 512], R[oi[hp]][:, :], siluz[:, s, hp * 512:(hp + 1) * 512], ALU.mult,
                       [BR[oi[hp]], Bsz[s][hp]], [BXa[s][2 * hp], BXa[s][2 * hp + 1]])
                reserved.clear()
                resv8.clear()
                oss = [BST("oss%d" % h) for h in range(4)]
                rk = "orstd%d" % s
                rstd_pow(stat[:, 40 + 4 * s:44 + 4 * s], stat[:, 8:12], 4, 1.0 / 256, oss, [BST(rk)])
                run_units(1)
                P.label = "C.core"
                for h in range(4):
                    TS(dgf[:, (s * 4 + h) * 128:(s * 4 + h + 1) * 128], identb[:, :], stat[:, 40 + 4 * s + h:41 + 4 * s + h],
                       [Bid, BST(rk)], [BolatT[s * 2 + h // 2]])
                run_units(4)
                run_jit(3)

        def phase_XaT():
            P.label = "C.XaT"
            for s in range(2):
                for c4 in range(2):
                    r, Br = nextR()
                    for cc in range(4):
                        c = c4 * 4 + cc
                        h = c // 2
                        MM(r[:, cc * 128:(cc + 1) * 128], Xa[:, s, c * 128:(c + 1) * 128],
                           dgf[:, (s * 4 + h) * 128:(s * 4 + h + 1) * 128], True, True,
                           [BXa[s][h], BolatT[s * 2 + h // 2]], [Br])
                    CP(alt(), XaT[:, c4 * 4:(c4 + 1) * 4, s * 128:(s + 1) * 128], r[:, :].rearrange("p (c t) -> p c t", c=4),
                       [Br], [BXaT[s]])

        def mla_norm_cache(c0, n, meta=False):
            r, Br = nextR()
            MM(r[:, 0:n], cf[:, C_ONES:C_ONES + 128], ckvsq[:, 0:n], True, True, [Bcf, Bckvsq], [Br])
            rstd_chain(rstdkv[:, 0:n], r[:, 0:n], "kv", 1.0 / 128, [Br], [Brkv])
            if meta:
                kb = [Bcache[0]]
                dstT = ckvnTm[:, 0:n]
            else:
                kb = [Bcache[1 + (c0 - 16) // 128], Bcache[2 + (c0 - 16) // 128]]
                dstT = ckvnT[:, c0:c0 + n]
            TT(dstT, ckvT[:, 0:n], rstdkv[:, 0:n], ALU.mult, [BckvT, Brkv], kb)

        def mla_cache_tok(c0, n, meta=False):
            kb = [Bcache[0]] if meta else [Bcache[1 + (c0 - 16) // 128], Bcache[2 + (c0 - 16) // 128]]
            subs = [(0, 16)] if meta else [(0, 128), (1, 128)]
            for (s, rows) in subs:
                tb, Btb = nextTB()
                src = ckvnTm[:, 0:rows] if meta else ckvnT[:, c0 + s * 128:c0 + s * 128 + rows]
                TR(tb[0:rows, 0:128], src, 128, [kb[s]], [Btb])
                blk = 0 if meta else 1 + (c0 - 16) // 128 + s
                CP(alt(), ckvtok[0:rows, blk, 0:128], tb[0:rows, 0:128], [Btb], [kb[s]])

        units = []

        def run_units(k):
            lab = P.label
            P.label = "D.qproj"
            while k > 0 and units:
                units.pop(0)()
                k -= 1
            P.label = lab

        def make_qproj_units(ri):
            n = T
            wq = {}

            def getw():
                if "w" not in wq:
                    wq["w"] = ring_acquire(WUQ)
                return wq["w"]

            def rope(pr):
                w, Bw = getw()
                a = 1 if pr % 2 == 0 else 3
                r1, Br1 = nextU()
                for k2 in range(2):
                    MM(r1[:, 0:n], w[:, k2 * 2048 + 1024 + pr * 128:k2 * 2048 + 1024 + (pr + 1) * 128], cqnT[:, k2, :],
                       k2 == 0, k2 == 1, [Bw] + Bcqn, [Br1])
                for k2 in range(2):
                    MM(r1[:, n:2 * n], w[:, k2 * 2048 + 1536 + pr * 128:k2 * 2048 + 1536 + (pr + 1) * 128], cqnT[:, k2, :],
                       k2 == 0, k2 == 1, [Bw] + Bcqn, [Br1])
                TT(rt2[a // 2][:, :], r1[:, :], ropeb[ri][:, :, :].rearrange("p c t -> p (c t)"), ALU.mult, [Br1, Brope[ri]], [Brt2[a // 2]])
                TT(qropeZ[0:64, 2 * pr, :], rt2[a // 2][0:64, 0:n], rt2[a // 2][0:64, n:2 * n], ALU.add, [Brt2[a // 2]], [Bqrope[2 * pr]], eng="pool")
                TT(qropeZ[64:128, 2 * pr + 1, :], rt2[a // 2][64:128, 0:n], rt2[a // 2][64:128, n:2 * n], ALU.add, [Brt2[a // 2]],
                   [Bqrope[2 * pr + 1]], eng="pool")

            def q_nope_pair(pr):
                w, Bw = getw()
                r, Br = nextU()
                for u in range(2):
                    h = 2 * pr + u
                    for k2 in range(2):
                        MM(r[:, u * n:(u + 1) * n], w[:, k2 * 2048 + h * 128:k2 * 2048 + (h + 1) * 128], cqnT[:, k2, :], k2 == 0, k2 == 1,
                           [Bw] + Bcqn, [Br])
                qi = pr % 2
                CP("act", qnP[qi][:, :], r[:, :], [Br], [BqnP[qi]])

            def q_absorb_pair(pr):
                qi = pr % 2
                r, Br = nextU()
                for u in range(2):
                    h = 2 * pr + u
                    MM(r[:, u * n:(u + 1) * n], wukT[:, h, :], qnP[qi][:, u * n:(u + 1) * n], True, True, [Bwuk, BqnP[qi]], [Br])
                ACT(qabsT[:, 2 * pr:2 * pr + 2, :].rearrange("p h t -> p (h t)"), r[:, :], AF.Copy, [Br, Bvecs],
                    [Bqabs[2 * pr], Bqabs[2 * pr + 1]], scale=vecs[:, 12:13])

            def mk(f, *a):
                return lambda: f(*a)

            units.append(mk(rope, 0))
            units.append(mk(q_nope_pair, 0))
            units.append(mk(rope, 1))
            units.append(mk(q_nope_pair, 1))
            units.append(mk(q_absorb_pair, 0))
            units.append(mk(rope, 2))
            units.append(mk(q_nope_pair, 2))
            units.append(mk(q_absorb_pair, 1))
            units.append(mk(rope, 3))
            units.append(mk(q_nope_pair, 3))
            units.append(mk(q_absorb_pair, 2))
            units.append(mk(q_absorb_pair, 3))

        def make_gate_units(ui):
            n = T
            wg = {}

            def gate(which, cc):
                def f():
                    lab = P.label
                    P.label = "E.gates"
                    if cc == 0:
                        wg[which] = ring_acquire(9 if which == 0 else 11)
                    w, Bw = wg[which]
                    r, Br = nextU()
                    for kc in range(8):
                        MM(r[:, 0:n], w[:, kc * 512 + cc * 128: kc * 512 + (cc + 1) * 128], uT[ui][:, kc, 0:n], kc == 0, kc == 7,
                           [Bw] + BuT[ui], [Br])
                    if which == 0:
                        ACT(sa[:, cc, :], r[:, 0:n], AF.Sigmoid, [Br], [Bsa[cc]])
                    else:
                        ACT(sbg[:, cc, :], r[:, 0:n], AF.Sigmoid, [Br], [Bsb[cc]])
                    P.label = lab
                return f

            for which in range(2):
                for cc in range(4):
                    units.append(gate(which, cc))

        def phase_D(ui, ri, st):
            c0 = 16 + st * T
            n = T
            run_units(100)
            P.label = "D.attn"
            tasks = []
            for h in range(8):
                tasks.append((h, "meta", 0))
                for jp in range(st):
                    tasks.append((h, "pair", jp))
                tasks.append((h, "diag", 0))
            state = {}
            started = {}

            def qk1(r, Br, h, kT_ap, rT_ap, blk, rcol, q0, nq, masked=False):
                MM(r[:, rcol:rcol + nq], kT_ap, qabsT[:, h, q0:q0 + nq], True, False, [Bcache[blk], Bqabs[h]], [Br])
                MM(r[:, rcol:rcol + nq], rT_ap, qropeZ[:, h, q0:q0 + nq], False, not masked, [Bcache[blk], Bqrope[h]], [Br])
                if masked:
                    MM(r[:, rcol:rcol + 128], identb[:, :], trib[:, :], False, True, [Bid, Btri], [Br])

            def emit_qk(t):
                h, kind, jp = t
                r, Br = nextR()
                p, Bp = nextPT()
                if kind == "meta":
                    qk1(r, Br, h, ckvnTm[:, :], kropeTm[:, :], 0, 0, 0, T)
                    ACT(p[:, 0:T], r[:, 0:T], AF.Exp, [Br], [Bp], scale=ATT_SCALE)
                elif kind == "pair":
                    for u in range(2):
                        j = 2 * jp + u
                        kc0 = 16 + 128 * j
                        qk1(r, Br, h, ckvnT[:, kc0:kc0 + 128], kropeT[:, kc0:kc0 + 128], 1 + j, u * 256, 0, T)
                    ACT(p[:, :], r[:, :], AF.Exp, [Br], [Bp], scale=ATT_SCALE)
                else:
                    j0 = 2 * st
                    kc0 = 16 + 128 * j0
                    qk1(r, Br, h, ckvnT[:, kc0:kc0 + 128], kropeT[:, kc0:kc0 + 128], 1 + j0, 0, 0, T, masked=True)
                    qk1(r, Br, h, ckvnT[:, kc0 + 128:kc0 + 256], kropeT[:, kc0 + 128:kc0 + 256], 2 + j0, 384, 128, 128, masked=True)
                    ACT(p[:, 0:256], r[:, 0:256], AF.Exp, [Br], [Bp], scale=ATT_SCALE)
                    ACT(p[:, 384:512], r[:, 384:512], AF.Exp, [Br], [Bp], scale=ATT_SCALE)
                state[t] = (p, Bp)

            def pv(h, p, Bp, pcol, sq, blk, last):
                ai = 2 * (h % 2) + sq
                MM(A4[ai][:, 0:129], p[:, pcol:pcol + 128], ckvtok[:, blk, 0:129], not started.get((h, sq), False), last,
                   [Bp, Bcache[blk]], [BA4[ai]])
                started[(h, sq)] = True

            def emit_pv(t):
                h, kind, jp = t
                p, Bp = state.pop(t)
                if kind == "meta":
                    for sq in range(2):
                        pv(h, p, Bp, sq * 128, sq, 0, False)
                elif kind == "pair":
                    for u in range(2):
                        j = 2 * jp + u
                        for sq in range(2):
                            pv(h, p, Bp, u * 256 + sq * 128, sq, 1 + j, False)
                else:
                    j0 = 2 * st
                    pv(h, p, Bp, 0, 0, 1 + j0, True)
                    pv(h, p, Bp, 128, 1, 1 + j0, False)
                    pv(h, p, Bp, 384, 1, 2 + j0, True)
                    for sq in range(2):
                        ai = 2 * (h % 2) + sq
                        k = "rec%d" % ai
                        col = 16 + sq * 8 + h
                        P.add("dve", lambda e, ai=ai, col=col: e.reciprocal(out=stat[:, col:col + 1], in_=A4[ai][:, 128:129]),
                              [BA4[ai]], [BST(k)])
                        TS(olat[:, sq, h, :], A4[ai][:, 0:128], stat[:, col:col + 1], [BA4[ai], BST(k)], [Bolat[sq][h]])

            DEPTH = 3
            for i, t in enumerate(tasks):
                if i == 2:
                    phase_XaT()
                    P.label = "D.attn"
                run_jit(1)
                if i == 3:
                    flush_sp()
                emit_qk(t)
                if i >= DEPTH:
                    emit_pv(tasks[i - DEPTH])
            for t in tasks[-DEPTH:]:
                emit_pv(t)
            run_jit(1000)
            P.label = "D.up"
            wz = {}

            def up_a(h):
                if h % 4 == 0:
                    wz["w"] = ring_acquire(7 + h // 4)
                w, Bw = wz["w"]
                tb, Btb = nextTB()
                for sq in range(2):
                    TR(tb[:, sq * 128:(sq + 1) * 128], olat[:, sq, h, :], 128, [Bolat[sq][h]], [Btb])
                CP("act", olatT[:, h, :], tb[:, 0:T], [Btb], [BolatT[h]])
                r, Br = proj_fm(w, Bw, (h % 4) * 128, 128, uT[ui], BuT[ui], n)
                si = h % 2
                ACT(siluT[si][:, :], r[:, 0:n], AF.Silu, [Br], [BsiluT[si]])

            def up_b(h):
                si = h % 2
                r2, Br2 = nextR()
                MM(r2[:, 0:n], wuv[:, h, :], olatT[:, h, :], True, True, [Bwuv, BolatT[h]], [Br2])
                TT(XbT[:, h, :], r2[:, 0:n], siluT[si][:, :], ALU.mult, [Br2, BsiluT[si]], [BXbT[h]])

            for h in range(8):
                up_a(h)
                if h >= 1:
                    up_b(h - 1)
            up_b(7)

        def phase_E(xi, ui, b, st, pre_gates=False):
            n = T
            P.label = "E.gates"
            for hf in range(2):
                if not (pre_gates and hf == 0):
                    w, Bw = ring_acquire(9 + hf)
                    for cc in range(4):
                        r, Br = proj_fm(w, Bw, cc * 128, 128, uT[ui], BuT[ui], n)
                        ACT(sa[:, cc, :], r[:, 0:n], AF.Sigmoid, [Br], [Bsa[cc]])
                w, Bw = ring_acquire(13 + hf)
                for cc in range(4):
                    r, Br = proj_fm(w, Bw, cc * 128, 128, XaT, BXaT, n)
                    TT(sa[:, cc, :], r[:, 0:n], sa[:, cc, :], ALU.mult, [Br, Bsa[cc]], [Bsa[cc]])
                if not (pre_gates and hf == 0):
                    w, Bw = ring_acquire(11 + hf)
                    for cc in range(4):
                        r, Br = proj_fm(w, Bw, cc * 128, 128, uT[ui], BuT[ui], n)
                        ACT(sbg[:, cc, :], r[:, 0:n], AF.Sigmoid, [Br], [Bsb[cc]])
                w, Bw = ring_acquire(15 + hf)
                for cc in range(4):
                    r, Br = proj_fm(w, Bw, cc * 128, 128, XbT, BXbT, n)
                    TT(sbg[:, cc, :], r[:, 0:n], sbg[:, cc, :], ALU.mult, [Br, Bsb[cc]], [Bsb[cc]])
                    TT(mT[:, hf * 4 + cc, :], sa[:, cc, :], sbg[:, cc, :], ALU.add, [Bsa[cc], Bsb[cc]], [BmT[hf * 4 + cc]])
            P.label = "E.wout"
            for hf in range(2):
                w, Bw = ring_acquire(17 + hf)
                for s in range(2):
                    r, Br = nextR()
                    for kc in range(8):
                        MM(r[:, :], mT[:, kc, s * 128:(s + 1) * 128], w[:, kc * 512:(kc + 1) * 512], kc == 0, kc == 7,
                           [Bw] + BmT, [Br])
                    xs = xbuf[xi][:, s, hf * 512:(hf + 1) * 512]
                    TT(xs, xs, r[:, :], ALU.add, [Bx[xi][s], Br], [Bx[xi][s]])
            P.label = "E.final"
            for s in range(2):
                k = "f%d" % s
                ACT(junk[:, :], xbuf[xi][:, s, :], AF.Square, [Bx[xi][s]], [Bjunk, BST(k)], accum=stat[:, 32 + s:33 + s])
                rstd_pow(stat[:, 34 + s:35 + s], stat[:, 32 + s:33 + s], 1, 1.0 / D, [BST(k)], [BST(k + "r")])
                oi = ctr["o"] % 2
                ctr["o"] += 1
                STT(obuf[oi][:, :], xbuf[xi][:, s, :], stat[:, 34 + s:35 + s], fgb[:, :], ALU.mult, ALU.mult,
                    [Bx[xi][s], BST(k + "r"), Bfgb], [Bo[oi]])
                def _store(b=b, st=st, s=s, oi=oi):
                    out_dmas.append(DMA("sp", y_d[b, st * T + s * 128: st * T + (s + 1) * 128, :], obuf[oi][:, :], [Bo[oi]], [], "o%d" % oi))
                deferred_sp.append(_store)

        out_dmas = []
        deferred_sp = []

        def flush_sp():
            while deferred_sp:
                deferred_sp.pop(0)()

        def load_x(xi, ri, b, st):
            DMA("sp", xbuf[xi][:, :, :], x_d[b, st * T:(st + 1) * T, :].rearrange("(s p) d -> p s d", p=128), [], Bx[xi], "x%d" % xi)
            c0 = 16 + st * T
            DMA("sp", ropeb[ri][:, :, :], rope_d[:, :, c0:c0 + T], [], [Brope[ri]], "rope%d" % ri)

        per_tile = [0, 1, 2, 3, 4, 5, 6, WUQ, 7, 8, 9, 13, 11, 15, 10, 14, 12, 16, 17, 18]
        ring_plan([1, 2, 3, 4])
        per_tile_pg = [0, 1, 2, 3, 4, 5, 6, WUQ, 9, 11, 7, 8, 13, 15, 10, 14, 12, 16, 17, 18]
        for b in range(NB):
            for st in range(NST):
                ring_plan(per_tile if (b == 0 and st == 0) else per_tile_pg)

        if stage >= 1:
            DMA("sp", xbuf[0][0:16, 0, :], meta_d, [], Bx[0], "x0")
            DMA("sp", ropeb[0][:, :, 0:16], rope_d[:, :, 0:16], [], [Brope[0]], "rope0")
            phase_norm_T(0, 0, [16], meta=True)
            phase_B(0, 0, 0, 16, meta=True, hook=lambda: run_jit(1))
            phase_C_gate(16, 1, meta=True)
            phase_C_meta()
            DMA("sp", sm_d, S32[:, :, :].rearrange("p h v -> p (h v)"), BS32, [BSm], "smw")
            run_jit(2)

        tiles = [(b, st) for b in range(NB) for st in range(NST)]
        if stage >= 1.2:
            load_x(0, 1, tiles[0][0], tiles[0][1])
        if stage >= 1.5:
            phase_norm_T(0, 0, [128, 128])
        for g, (b, st) in enumerate(tiles):
            if stage < 2:
                break
            xi = g % 2
            ui = g % 2
            ri = (g + 1) % 2
            if st == 0:
                DMA("sp", S32[:, :, :].rearrange("p h v -> p (h v)"), sm_d, [BSm], BS32, "smr")
                for h in range(4):
                    CP("act", S16[0][:, h, :], S32[:, h, :], [BS32[h]], [BS16[0][h]])
            phase_B(ui, ri, 16 + st * T, T, hook=(lambda: run_jit(4)) if g == 0 else None)
            if stage < 3:
                break
            phase_C_gz(ui)
            phase_C_gate(T, 4)
            make_qproj_units(ri)
            if g > 0:
                make_gate_units(ui)
            phase_C_core(ui, st, 0)
            if stage < 4:
                break
            if g + 1 < len(tiles):
                load_x((g + 1) % 2, (g + 2) % 2, tiles[g + 1][0], tiles[g + 1][1])
                phase_norm_T((g + 1) % 2, (g + 1) % 2, [128, 128], part=1)
            phase_D(ui, ri, st)
            if stage < 5:
                break
            if g + 1 < len(tiles):
                phase_norm_T((g + 1) % 2, (g + 1) % 2, [128, 128], part=2)
            phase_E(xi, ui, b, st, pre_gates=(g > 0))
        flush_sp()
        fin = P.add("sp", None)
        for d in out_dmas[-4:]:
            fin.deps.add(d)
        for k, I in P.last_dma.items():
            if str(k).startswith("tap_"):
                fin.deps.add(I)

        cnt, dcnt = P.finalize()
        sbuf_left = nc.sbuf_bytes_remaining
        sems = {}
        for k in P.sem_keys:
            sems[k] = es.enter_context(nc.semaphore("s_%s_%s" % k))
        with nc.Block() as block:
            P.emit(nc, block, sems)
    return nc, (cnt, dcnt, len(P.ins), sbuf_left, P)


def _chunk(Wc):
    return np.ascontiguousarray(Wc.reshape(8, 128, -1).transpose(1, 0, 2).reshape(128, -1))


def _rope_tables():
    inv = (1.0 / (np.float32(10000.0) ** (np.arange(0, 64, 2, dtype=np.float32) / np.float32(64)))).astype(np.float32)
    ang = (np.arange(LTOT, dtype=np.float32)[:, None] * inv[None, :]).astype(np.float32)
    cos = np.cos(ang).astype(np.float32).T
    sin = np.sin(ang).astype(np.float32).T
    cos64 = np.concatenate([cos, cos], 0)
    sin64 = np.concatenate([-sin, sin], 0)
    tab = np.zeros((128, 2, LTOT), np.float32)
    tab[:, 0, :] = np.concatenate([cos64, cos64], 0)
    tab[:, 1, :] = np.concatenate([sin64, sin64], 0)
    return tab


def _consts():
    cf = np.zeros((128, C_W), np.float32)
    cf[:, C_ID:C_ID + 128] = np.eye(128, dtype=np.float32)
    cf[:, C_ONES:C_ONES + 128] = 1.0
    reset = np.ones((128, 256), np.float32)
    reset[:, ::128] = 0.0
    cf[:, C_RESET:C_RESET + 256] = reset
    j = np.arange(128)[:, None]
    i = np.arange(128)[None, :]
    gm = (j <= i).astype(np.float32)
    cf[:, C_GMASK:C_GMASK + 512] = np.tile(gm, (1, 4))
    cf[:, C_TRI:C_TRI + 128] = (j <= i).astype(np.float32)

    return cf


def prep_weights(meta_tokens, norm_g, w_in, gla_gate_w, gla_gate_b, gla_norm_g, gla_proj, mla_q_norm_g, mla_w_uq,
                 mla_kv_norm_g, mla_w_ukv, mla_proj, w_out, final_norm_g):
    f = np.float32
    W = np.asarray(w_in[0], f)
    cuts = np.cumsum([512, 512, 1024, 16, 1024, 256, 128, 64, 1024, 1024, 1024])
    o_q, o_k, o_v, o_lr, o_z, o_cq, o_ckv, o_kr, o_mz, o_gg, o_gm = [0] + list(cuts[:-1])
    kr = W[:, o_kr:o_kr + 64]
    krp = np.concatenate([kr[:, 32:64], kr[:, 0:32]], 1)
    chunks = [W[:, o_q:o_q + 512], W[:, o_k:o_k + 512],
              np.concatenate([W[:, o_cq:o_cq + 256], W[:, o_ckv:o_ckv + 128], kr, kr], 1),
              W[:, o_v:o_v + 512], W[:, o_v + 512:o_v + 1024],
              W[:, o_z:o_z + 512], W[:, o_z + 512:o_z + 1024],
              W[:, o_mz:o_mz + 512], W[:, o_mz + 512:o_mz + 1024],
              W[:, o_gg:o_gg + 512], W[:, o_gg + 512:o_gg + 1024],
              W[:, o_gm:o_gm + 512], W[:, o_gm + 512:o_gm + 1024]]
    for M in (gla_proj, mla_proj, w_out):
        M = np.asarray(M[0], f)
        chunks += [M[:, 0:512], M[:, 512:1024]]
    wst = np.zeros((NCHUNK, 128, 4096), f)
    for c, Wc in enumerate(chunks):
        wst[c] = _chunk(np.ascontiguousarray(Wc))
    Wq = np.asarray(mla_w_uq[0], f).reshape(256, 8, 192)
    nope = Wq[:, :, 0:128].reshape(256, 1024)
    rope = Wq[:, :, 128:192]
    ropep = np.concatenate([rope[:, :, 32:64], rope[:, :, 0:32]], 2)
    Wq_ext = np.concatenate([nope, rope.reshape(256, 512), ropep.reshape(256, 512)], 1)
    wst[WUQ] = np.ascontiguousarray(Wq_ext.reshape(2, 128, 2048).transpose(1, 0, 2).reshape(128, 4096))
    wsm_src = np.concatenate([krp, krp, W[:, o_lr:o_lr + 16]], 1)
    wsm = np.ascontiguousarray(wsm_src.reshape(8, 128, 144).transpose(1, 0, 2).reshape(128, 1152))
    Wkv = np.asarray(mla_w_ukv[0], f).reshape(128, 8, 256)
    wukT = np.ascontiguousarray(Wkv[:, :, 0:128].transpose(2, 1, 0).reshape(128, 1024))
    wuv = np.ascontiguousarray(Wkv[:, :, 128:256].reshape(128, 1024))
    wuk = np.concatenate([wukT, wuv], 1)
    gwb = np.concatenate([np.asarray(gla_gate_w[0], f), np.asarray(gla_gate_b[0], f)[None, :]], 0)
    vecs = np.zeros((128, 16), f)
    vecs[:, 0:8] = np.asarray(norm_g[0], f).reshape(8, 128).T
    vecs[:, 8:10] = np.asarray(gla_norm_g[0], f).reshape(2, 128).T
    vecs[:, 10:12] = np.asarray(mla_q_norm_g[0], f).reshape(2, 128).T
    vecs[:, 12] = np.asarray(mla_kv_norm_g[0], f)
    fgb = np.ascontiguousarray(np.broadcast_to(np.asarray(final_norm_g, f)[None, :], (128, D)))
    return {"meta": np.ascontiguousarray(np.asarray(meta_tokens, f)), "wst": wst, "wsm": wsm, "wuk": wuk, "gwb": gwb,
            "vecs": vecs, "fgb": fgb, "cf32": _consts(), "rope": _rope_tables()}


_CACHE = {}


def kernel(x, meta_tokens, norm_g, w_in, gla_gate_w, gla_gate_b, gla_norm_g, gla_proj, mla_q_norm_g, mla_w_uq,
           mla_kv_norm_g, mla_w_ukv, mla_proj, w_out, final_norm_g):
    ncores = 8
    x = np.asarray(x, np.float32)
    B = x.shape[0]
    NB = B // ncores
    shared = prep_weights(meta_tokens, norm_g, w_in, gla_gate_w, gla_gate_b, gla_norm_g, gla_proj, mla_q_norm_g,
                          mla_w_uq, mla_kv_norm_g, mla_w_ukv, mla_proj, w_out, final_norm_g)
    key = (NB, SEQ // T)
    if key not in _CACHE:
        _CACHE[key] = build(NB, SEQ // T)[0]
    nc = _CACHE[key]
    in_maps = []
    for c in range(ncores):
        m = dict(shared)
        m["x"] = np.ascontiguousarray(x[c * NB:(c + 1) * NB])
        in_maps.append(m)
    res = run_bass_kernel_spmd(nc, in_maps, core_ids=list(range(ncores)))
    return np.concatenate([r["y"] for r in res.results], axis=0)
```
